# Optimizing a Trainium2 kernel written in Bass

```python
import math
import jax, jax.numpy as jnp
from jax import lax
import numpy as np

D_MODEL = 2048
BATCH = 4
SEQ = 2048
DEPTH = 1
DEC_BATCH = 128
DEC_SEQ = 8
PAST_LEN = 16384
PAGE_SIZE = 128

D_MIX = D_MODEL
SSD_WIDTH = D_MIX // 2
SSD_HEADDIM = 64
SSD_HEADS = SSD_WIDTH // SSD_HEADDIM
SSD_GROUPS = 2
SSD_STATE = 128
SSD_CHUNK = 128
CONV_K = 4
CONV_DIM = SSD_WIDTH + 2 * SSD_GROUPS * SSD_STATE
GM_WIDTH = D_MIX - SSD_WIDTH
GM_HEAD = 128
GM_HEADS = GM_WIDTH // GM_HEAD
GM_CHUNK = 128
D_FF = 4 * D_MODEL
D_IN_PROJ = SSD_WIDTH + CONV_DIM + SSD_HEADS + 2 * GM_WIDTH
ALPHA = (2.0 * DEPTH) ** 0.25
BETA = (8.0 * DEPTH) ** -0.25
LN_EPS = 1e-5

kernel_name = "hymba_ssd_gmlp_deepnorm_adaln_step"


def layer_norm(x, g, b):
    xf = x.astype(jnp.float32)
    mu = jnp.mean(xf, -1, keepdims=True)
    var = jnp.mean(jnp.square(xf - mu), -1, keepdims=True)
    return ((xf - mu) * lax.rsqrt(var + LN_EPS) * g + b).astype(x.dtype)


def gated_group_rmsnorm(y, z, g):
    h = (y * jax.nn.silu(z)).astype(jnp.float32)
    shp = h.shape
    h = h.reshape(shp[:-1] + (SSD_GROUPS, shp[-1] // SSD_GROUPS))
    h = h * lax.rsqrt(jnp.mean(h * h, -1, keepdims=True) + LN_EPS)
    return (h.reshape(shp) * g).astype(y.dtype)


def causal_dwconv(xbc, buf, w, b):
    xp = jnp.concatenate([buf.astype(xbc.dtype), xbc], axis=1)
    y = lax.conv_general_dilated(xp, w[:, None, :].astype(xbc.dtype), window_strides=(1,), padding='VALID',
                                 dimension_numbers=('NWC', 'WIO', 'NWC'), feature_group_count=xbc.shape[-1])
    return jax.nn.silu(y + b), xp[:, -(CONV_K - 1):]


def ssd_scan(x, dt, A, Bm, Cm, D, s0):
    b, L, H, P = x.shape
    q = math.gcd(L, SSD_CHUNK)
    nc = L // q
    rep = H // SSD_GROUPS
    f32 = jnp.float32
    xc = x.astype(f32).reshape(b, nc, q, H, P)
    dtc = dt.astype(f32).reshape(b, nc, q, H)
    Bh = jnp.repeat(Bm.astype(f32), rep, axis=2).reshape(b, nc, q, H, -1)
    Ch = jnp.repeat(Cm.astype(f32), rep, axis=2).reshape(b, nc, q, H, -1)
    acum = jnp.cumsum(dtc * A.astype(f32), axis=2)
    seg = acum[:, :, :, None, :] - acum[:, :, None, :, :]
    causal = jnp.tril(jnp.ones((q, q), bool))[:, :, None]
    decay = jnp.exp(jnp.where(causal, seg, -jnp.inf))
    xdt = xc * dtc[..., None]
    scores = jnp.einsum('bcihn,bcjhn->bcijh', Ch, Bh) * decay
    y_diag = jnp.einsum('bcijh,bcjhp->bcihp', scores, xdt)
    decay_end = jnp.exp(acum[:, :, -1:, :] - acum)
    chunk_states = jnp.einsum('bcjhn,bcjhp->bchpn', Bh * decay_end[..., None], xdt)
    chunk_decay = jnp.exp(acum[:, :, -1, :])

    def step(s, inp):
        st, dc = inp
        return dc[:, :, None, None] * s + st, s

    s_final, s_in = lax.scan(step, s0.astype(f32),
                             (jnp.moveaxis(chunk_states, 1, 0), jnp.moveaxis(chunk_decay, 1, 0)))
    s_in = jnp.moveaxis(s_in, 0, 1)
    y_off = jnp.einsum('bcihn,bchpn->bcihp', Ch, s_in) * jnp.exp(acum)[..., None]
    y = y_diag + y_off + D.astype(f32)[:, None] * xc
    return y.reshape(b, L, H, P).astype(x.dtype), s_final.astype(s0.dtype)


def chunk_spatial_gate(u, v, ln_g, ln_b, w_s, b_s):
    b, L, _ = u.shape
    q = min(GM_CHUNK, L)
    nc = L // q
    u = jax.nn.gelu(u)
    v = layer_norm(jax.nn.gelu(v), ln_g, ln_b)
    w = jnp.where(jnp.tril(jnp.ones((q, q), bool)), w_s[:, :q, :q], 0.0)
    vc = v.reshape(b, nc, q, GM_HEADS, GM_HEAD)
    mixed = jnp.einsum('hij,bcjhd->bcihd', w, vc) + jnp.transpose(b_s[:, :q])[None, None, :, :, None]
    return u * mixed.reshape(b, L, GM_WIDTH).astype(u.dtype), v


def hybrid_mixer(h, conv_buf, ssm_state, w_in, conv_w, conv_b, dt_bias, a_log, d_skip, ssd_norm_g,
                 gm_ln_g, gm_ln_b, gm_w_s, gm_b_s, w_out):
    b, L, _ = h.shape
    proj = h @ w_in
    i1 = SSD_WIDTH
    i2 = i1 + CONV_DIM
    i3 = i2 + SSD_HEADS
    i4 = i3 + GM_WIDTH
    z, xbc, dt_raw, u, v = jnp.split(proj, [i1, i2, i3, i4], axis=-1)
    xbc, new_buf = causal_dwconv(xbc, conv_buf, conv_w, conv_b)
    xs, Bm, Cm = jnp.split(xbc, [SSD_WIDTH, SSD_WIDTH + SSD_GROUPS * SSD_STATE], axis=-1)
    dt = jax.nn.softplus((dt_raw + dt_bias).astype(jnp.float32))
    A = -jnp.exp(a_log.astype(jnp.float32))
    y, s_new = ssd_scan(xs.reshape(b, L, SSD_HEADS, SSD_HEADDIM), dt, A,
                        Bm.reshape(b, L, SSD_GROUPS, SSD_STATE), Cm.reshape(b, L, SSD_GROUPS, SSD_STATE),
                        d_skip, ssm_state)
    y_ssd = gated_group_rmsnorm(y.reshape(b, L, SSD_WIDTH), z, ssd_norm_g)
    y_gm, v_rows = chunk_spatial_gate(u, v, gm_ln_g, gm_ln_b, gm_w_s, gm_b_s)
    out = jnp.concatenate([y_ssd, y_gm], axis=-1) @ w_out
    return out, new_buf, s_new, v_rows


def decoder_layer(x, c, conv_buf, ssm_state, w_mod, b_mod, w_in, conv_w, conv_b, dt_bias, a_log, d_skip,
                  ssd_norm_g, gm_ln_g, gm_ln_b, gm_w_s, gm_b_s, w_out, ln_mix_g, ln_mix_b,
                  w_ff1, w_ff2, ln_ffn_g, ln_ffn_b):
    mod = (jax.nn.silu(c) @ w_mod + b_mod)[:, None, :]
    sh_m, sc_m, g_m, sh_f, sc_f, g_f = jnp.split(mod, 6, axis=-1)
    h = x * (1 + sc_m) + sh_m
    mix, new_buf, s_new, v_rows = hybrid_mixer(h, conv_buf, ssm_state, w_in, conv_w, conv_b, dt_bias, a_log,
                                               d_skip, ssd_norm_g, gm_ln_g, gm_ln_b, gm_w_s, gm_b_s, w_out)
    x = layer_norm(ALPHA * x + (1 + g_m) * mix, ln_mix_g, ln_mix_b)
    h = x * (1 + sc_f) + sh_f
    f = jnp.square(jax.nn.relu(h @ w_ff1)) @ w_ff2
    x = layer_norm(ALPHA * x + (1 + g_f) * f, ln_ffn_g, ln_ffn_b)
    return x, new_buf, s_new, v_rows


def setup_inputs(seed: int = 0) -> dict:
    key = jax.random.key(seed)
    ks = iter(jax.random.split(key, 40))
    f32 = jnp.float32

    def nrm(shape, s):
        return jax.random.normal(next(ks), shape, f32) * s

    dt0 = jnp.exp(jax.random.uniform(next(ks), (DEPTH, SSD_HEADS), f32, math.log(1e-3), math.log(1e-1)))
    dt_bias = dt0 + jnp.log(-jnp.expm1(-dt0))
    a_log = jnp.log(jax.random.uniform(next(ks), (DEPTH, SSD_HEADS), f32, 1.0, 16.0))
    return {
        "x_prompt": nrm((BATCH, SEQ, D_MODEL), 1.0),
        "x_sample": nrm((DEC_BATCH, DEC_SEQ, D_MODEL), 1.0),
        "state_ssm": nrm((DEPTH, DEC_BATCH, SSD_HEADS, SSD_HEADDIM, SSD_STATE), 0.5),
        "state_conv": nrm((DEPTH, DEC_BATCH, CONV_K - 1, CONV_DIM), 1.0),
        "c_prompt": nrm((BATCH, D_MODEL), 1.0),
        "c_sample": nrm((DEC_BATCH, D_MODEL), 1.0),
        "ln_in_g": 1.0 + nrm((D_MODEL,), 0.02),
        "ln_in_b": nrm((D_MODEL,), 0.02),
        "w_mod": nrm((DEPTH, D_MODEL, 6 * D_MODEL), 0.5 * D_MODEL ** -0.5),
        "b_mod": nrm((DEPTH, 6 * D_MODEL), 0.02),
        "w_in": nrm((DEPTH, D_MODEL, D_IN_PROJ), D_MODEL ** -0.5),
        "conv_w": nrm((DEPTH, CONV_K, CONV_DIM), CONV_K ** -0.5),
        "conv_b": nrm((DEPTH, CONV_DIM), 0.02),
        "dt_bias": dt_bias,
        "a_log": a_log,
        "d_skip": 1.0 + nrm((DEPTH, SSD_HEADS), 0.02),
        "ssd_norm_g": 1.0 + nrm((DEPTH, SSD_WIDTH), 0.02),
        "gm_ln_g": 1.0 + nrm((DEPTH, GM_WIDTH), 0.02),
        "gm_ln_b": nrm((DEPTH, GM_WIDTH), 0.02),
        "gm_w_s": nrm((DEPTH, GM_HEADS, GM_CHUNK, GM_CHUNK), GM_CHUNK ** -0.5),
        "gm_b_s": 1.0 + nrm((DEPTH, GM_HEADS, GM_CHUNK), 0.02),
        "w_out": nrm((DEPTH, D_MIX, D_MODEL), BETA * D_MIX ** -0.5),
        "ln_mix_g": 1.0 + nrm((DEPTH, D_MODEL), 0.02),
        "ln_mix_b": nrm((DEPTH, D_MODEL), 0.02),
        "w_ff1": nrm((DEPTH, D_MODEL, D_FF), D_MODEL ** -0.5),
        "w_ff2": nrm((DEPTH, D_FF, D_MODEL), BETA * D_FF ** -0.5),
        "ln_ffn_g": 1.0 + nrm((DEPTH, D_MODEL), 0.02),
        "ln_ffn_b": nrm((DEPTH, D_MODEL), 0.02),
    }


def reference(x_prompt, x_sample, state_ssm, state_conv, c_prompt, c_sample, ln_in_g, ln_in_b,
              w_mod, b_mod, w_in, conv_w, conv_b, dt_bias, a_log, d_skip, ssd_norm_g, gm_ln_g, gm_ln_b,
              gm_w_s, gm_b_s, w_out, ln_mix_g, ln_mix_b, w_ff1, w_ff2, ln_ffn_g, ln_ffn_b):
    bp = x_prompt.shape[0]
    xp = layer_norm(x_prompt, ln_in_g, ln_in_b)
    xs = layer_norm(x_sample, ln_in_g, ln_in_b)
    ssm_p, conv_p, ssm_s, conv_s, v_s = [], [], [], [], []
    for l in range(DEPTH):
        prm = (w_mod[l], b_mod[l], w_in[l], conv_w[l], conv_b[l], dt_bias[l], a_log[l], d_skip[l],
               ssd_norm_g[l], gm_ln_g[l], gm_ln_b[l], gm_w_s[l], gm_b_s[l], w_out[l], ln_mix_g[l], ln_mix_b[l],
               w_ff1[l], w_ff2[l], ln_ffn_g[l], ln_ffn_b[l])
        zero_buf = jnp.zeros((bp, CONV_K - 1, CONV_DIM), xp.dtype)
        zero_ssm = jnp.zeros((bp, SSD_HEADS, SSD_HEADDIM, SSD_STATE), state_ssm.dtype)
        xp, buf_p, s_p, _ = decoder_layer(xp, c_prompt, zero_buf, zero_ssm, *prm)
        xs, buf_s, s_s, v_rows = decoder_layer(xs, c_sample, state_conv[l], state_ssm[l], *prm)
        ssm_p.append(s_p)
        conv_p.append(buf_p)
        ssm_s.append(s_s)
        conv_s.append(buf_s)
        v_s.append(v_rows)
    return (xp, xs, jnp.stack(ssm_p), jnp.stack(conv_p), jnp.stack(ssm_s), jnp.stack(conv_s), jnp.stack(v_s))
```

```python
import math
from contextlib import ExitStack
import numpy as np
import concourse.bass as bass
import concourse.mybir as mybir
from concourse.bass_utils import run_bass_kernel_spmd

F32 = mybir.dt.float32
BF16 = mybir.dt.bfloat16
AF = mybir.ActivationFunctionType
ALU = mybir.AluOpType

D = 2048
NCH = 16
DFF = 8192
DIN = 4624
ALPHA = 2.0 ** 0.25
EPS = 1e-5
NT = 3
RING = 6
CGW = 528

PC_GIN, PC_BIN, PC_BMOD, PC_CW, PC_CB, PC_NG, PC_GMIX, PC_BMIX, PC_GFFN, PC_BFFN = 0, 16, 32, 128, 176, 188, 196, 212, 228, 244
PC_N = 260
PR_DTB, PR_ALOG, PR_DSK, PR_GG, PR_GB = 0, 16, 32, 48, 48 + 1024
PR_N = 48 + 2048
CS_ID, CS_U, CS_L, CS_ONE, CS_LB, CS_BO, CS_BSEL = 0, 128, 256, 384, 512, 640, 768
CS_N = 784


class Op:
    __slots__ = ("eng", "fn", "deps", "is_dma", "sig", "sigval", "sem", "raw")


class Sched:
    ENGS = ["pe", "act", "dve", "pool", "sp"]

    def __init__(self):
        self.ops = {e: [] for e in self.ENGS}
        self.lastw = {}
        self.reads = {}
        self.dma_count = {e: 0 for e in self.ENGS}
        self.dma_prev = {}
        self.NDMA = 8
        self.pending = {}
        self.rd_dmas = []

    def add(self, eng, fn, reads=(), writes=(), dma=False):
        op = Op()
        op.eng, op.fn, op.is_dma, op.sig, op.sigval, op.sem = eng, fn, dma, False, 0, None
        deps = []
        raw = set()
        for k in reads:
            w = self.lastw.get(k)
            if w is not None:
                deps.append(w)
                raw.add(id(w))
            if isinstance(k, tuple) and k[0] in ("pf", "pb"):
                deps.extend(r for r in self.reads.get(k, ()) if r.eng != eng)
        for k in writes:
            w = self.lastw.get(k)
            if w is not None:
                deps.append(w)
            deps.extend(self.reads.get(k, ()))
        if self.pending.get(eng) is not None and (not dma or eng == "sp"):
            deps.append(self.pending[eng])
            self.pending[eng] = None
        if dma and len(reads) > 0:
            self.rd_dmas.append(op)
        if dma:
            op.sem = (eng, self.dma_count[eng] % self.NDMA)
            op.sigval = 16 * (self.dma_count[eng] // self.NDMA + 1)
            prev = self.dma_prev.get(op.sem)
            if prev is not None:
                deps.append(prev)
            self.dma_prev[op.sem] = op
            self.dma_count[eng] += 1
        seen = set()
        op.raw = raw
        op.deps = []
        for d in deps:
            if d is op or id(d) in seen:
                continue
            seen.add(id(d))
            op.deps.append(d)
        for k in writes:
            self.lastw[k] = op
            self.reads[k] = []
        for k in reads:
            if k in writes:
                continue
            lst = self.reads.setdefault(k, [])
            if not dma:
                lst[:] = [r for r in lst if r.is_dma or r.eng != eng]
            lst.append(op)
        self.ops[eng].append(op)
        return op

    def barrier(self, fn):
        deps = []
        for e in self.ENGS:
            for o in reversed(self.ops[e]):
                if not o.is_dma:
                    deps.append(o)
                    break
        deps.extend(self.rd_dmas)
        self.rd_dmas = []
        op = self.add("dve", fn)
        for d in deps:
            if d is not op and d not in op.deps:
                op.deps.append(d)
        for e in ["pe", "act", "pool", "sp"]:
            self.pending[e] = op
        return op

    def _skip(self, d, e, op=None):
        if d.is_dma or d.eng != e:
            return False
        if e == "pe":
            return True
        if e in ("act", "dve") and op is not None and id(d) not in op.raw:
            return True
        return False

    def finalize(self):
        for e in self.ENGS:
            for op in self.ops[e]:
                for d in op.deps:
                    if d.is_dma or self._skip(d, e, op):
                        continue
                    d.sig = True
        for e in self.ENGS:
            c = 0
            for op in self.ops[e]:
                if (not op.is_dma) and op.sig:
                    c += 1
                    op.sigval = c

    def emit(self, block, sems, dsems):
        engmap = {"pe": block.tensor, "act": block.scalar, "dve": block.vector,
                  "pool": block.gpsimd, "sp": block.sync}
        for e in self.ENGS:
            ops = self.ops[e]
            if not ops:
                continue

            def body(eng, e=e, ops=ops):
                waited = {}
                for op in ops:
                    for d in op.deps:
                        if d.is_dma:
                            key, sem, val = ("d",) + d.sem, dsems[d.sem], d.sigval
                        else:
                            if self._skip(d, e, op):
                                continue
                            key, sem, val = ("c", d.eng), sems[d.eng], d.sigval
                        if waited.get(key, 0) >= val:
                            continue
                        waited[key] = val
                        eng.wait_ge(sem, val)
                    ins = op.fn(eng)
                    if op.is_dma:
                        ins.then_inc(dsems[op.sem], 16)
                    elif op.sig:
                        ins.then_inc(sems[e], 1)

            engmap[e](body)


def build_program(nt=NT):
    nc = bass.Bass("TRN2", target_bir_lowering=False)

    def din(name, shape):
        return nc.dram_tensor(name, list(shape), F32, kind="ExternalInput").ap()

    def dout(name, shape):
        return nc.dram_tensor(name, list(shape), F32, kind="ExternalOutput").ap()

    xT_d = din("xT", [D, 1152])
    xpT_d = din("xpT", [D, 1024])
    gate_d = din("gate", [128, 1])
    cT_d = din("cT", [128, 16, 17])
    wmod_d = din("wmod", [24, 8, 128, 2, 512])
    wina_d = din("wina", [8, 8, 128, 2, 512])
    winb_d = din("winb", [1, 8, 128, 2, 528])
    wout_d = din("wout", [4, 8, 128, 2, 512])
    wff1_d = din("wff1", [16, 8, 128, 2, 512])
    wff2_d = din("wff2", [4, 32, 128, 2, 512])
    pcols_d = din("pcols", [128, PC_N])
    prow_d = din("prow", [128, PR_N])
    bsT_d = din("bsT", [128, 16])
    wsT_d = din("wsT", [128, 16, 128])
    cst_d = din("cst", [128, CS_N])
    colsel_d = din("colsel", [128, 16, 128])
    stT_d = din("stT", [16, 128, 1024])
    cvT_d = din("cvT", [128, 12, 16, 3])

    yT_d = dout("yT", [D, 1152])
    ssmT_d = dout("ssmT", [128, 1024])
    convT_d = dout("convT", [128, 12, 3])
    ssmsT_d = dout("ssmsT", [16, 128, 1024])
    convsT_d = dout("convsT", [128, 12, 16, 3])
    vs_d = dout("vs", [128, 1024])

    S = Sched()
    TBM = 128 * nt

    with ExitStack() as es:
        def sb(name, shape, dt=F32):
            return es.enter_context(nc.sbuf_tensor("s_" + name, list(shape), dt))

        def ps(name, shape, dt=F32):
            return es.enter_context(nc.psum_tensor(name, list(shape), dt))

        sems = {e: es.enter_context(nc.semaphore("s_" + e)) for e in Sched.ENGS}
        dsems = {(e, i): es.enter_context(nc.semaphore(f"d_{e}_{i}"))
                 for e in ["sp", "pool"] for i in range(S.NDMA)}

        cst = sb("cst", [128, CS_N])
        ident = cst[:, CS_ID:CS_ID + 128]
        Um = cst[:, CS_U:CS_U + 128]
        Lm = cst[:, CS_L:CS_L + 128]
        ones = cst[:, CS_ONE:CS_ONE + 128]
        LmB = cst[:, CS_LB:CS_LB + 128]
        BlkO = cst[:, CS_BO:CS_BO + 128]
        Bsel = cst[:, CS_BSEL:CS_BSEL + 16]
        identb = sb("identb", [128, 128], BF16)
        colsel = sb("colsel", [128, 16, 128], BF16)
        pcols = sb("pcols", [128, PC_N])
        prow = sb("prow", [128, PR_N])
        bsT = sb("bsT", [128, 16])
        WT = sb("WT", [128, 16, 128], BF16)
        gate = sb("gate", [128, 1])
        scT = sb("scT", [128, 16, 17], BF16)
        sclh = sb("sclh", [128, 16, 17])
        bish = sb("bish", [128, 16, 17])
        sclh2 = sb("sclh2", [128, 16, 17])
        bish2 = sb("bish2", [128, 16, 17])
        Gm = sb("Gm", [128, 16, 17])
        Gf = sb("Gf", [128, 16, 17])
        agin = sb("agin", [128, 64])
        Arow = sb("Arow", [128, 16])
        cvT = sb("cvT", [128, 12, 16, 3])
        convs = sb("convs", [128, 12, 16, 3])
        tail = sb("tail", [128, 12, 3])
        ST = sb("ST", [128, 1024])
        STb = sb("STb", [128, 1024], BF16)

        xA = sb("xA", [128, 16, TBM])
        hT = sb("hT", [128, 16, TBM], BF16)
        assert nt == 3
        AR = 4352 * nt
        arena = sb("arena", [128, AR])
        o_yy, o_gv, o_xc, o_xtm, o_mix = 0, 1024 * nt, 2048 * nt, 2048 * nt + 768 * nt, 2048 * nt + 1280 * nt
        yy = arena[:, o_yy:o_yy + 1024 * nt].rearrange("p (t f) -> p t f", f=1024)
        gv = arena[:, o_gv:o_gv + 1024 * nt].rearrange("p (t f) -> p t f", f=1024)
        xc = arena[:, o_xc:o_xc + 768 * nt].bitcast(BF16).rearrange("p (c t) -> p c t", t=TBM)
        x_tm = arena[:, o_xtm:o_xtm + 512 * nt].bitcast(BF16).rearrange("p (t f) -> p t f", f=1024)
        mixT = arena[:, o_mix:o_mix + 1024 * nt].bitcast(BF16).rearrange("p (c t) -> p c t", t=TBM)
        hid = arena[:, 0:4096 * nt].bitcast(BF16).rearrange("p (c t) -> p c t", t=TBM)
        stin = [arena[:, o_mix + i * 1024:o_mix + (i + 1) * 1024] for i in range(2)]
        snew = arena[:, o_mix + 2048:o_mix + 3072]
        modT = arena[:, 0:1632].rearrange("p (c k) -> p c k", k=17)
        wstmp = arena[:, 1632:1632 + 2048].rearrange("p (c k) -> p c k", k=128)
        ctmp = arena[:, 3680:3680 + 272].rearrange("p (c k) -> p c k", k=17)
        ctmp2 = arena[:, 3952:3952 + 272].rearrange("p (c k) -> p c k", k=17)
        dummy = sb("dummy", [128, 8])
        ring = [sb(f"ring{i}", [128, 2, CGW], BF16) for i in range(RING)]
        sq = [sb(f"sq{i}", [128, TBM], BF16) for i in range(2)]
        onesb = sb("onesb", [128, 128], BF16)
        mean = sb("mean", [128, TBM])
        rstd = sb("rstd", [128, TBM])
        vtmp = sb("vtmp", [128, TBM])
        xn = [sb(f"xn{i}", [128, TBM]) for i in range(2)]
        rl = [sb(f"rl{i}", [128, TBM]) for i in range(2)]
        stg2 = [sb(f"stg{i}", [128, 4, 3 + TBM]) for i in range(2)]
        sstg2 = [sb(f"sstg{i}", [128, 4, 16, 11]) for i in range(2)]
        cacc2 = [sb(f"cacc{i}", [128, TBM]) for i in range(2)]
        B_tm = sb("B_tm", [128, nt, 256], BF16)
        smt = [sb(f"sm{i}", [128, 12, 16]) for i in range(nt)]
        lh = [sb(f"lh{i}", [128, 4, 128]) for i in range(2)]
        Ee = [sb(f"Ee{i}", [128, 4, 128]) for i in range(2)]
        Mm = sb("Mm", [128, 16, 128], BF16)
        CBm = sb("CBm", [128, 2, 128], BF16)
        xdt = sb("xdt", [128, 1024], BF16)
        xdd = sb("xdd", [128, 1024], BF16)
        xDs = sb("xDs", [128, 1024], BF16)
        vnb = sb("vnb", [128, 1024], BF16)
        gut = sb("gut", [128, 512])
        ysb = sb("ysb", [128, 1024], BF16)
        bstt = sb("bstt", [128, nt, 12])
        bmvt = sb("bmvt", [128, nt, 2])
        ss2t = sb("ss2t", [128, nt, 4])
        ablk = sb("ablk", [128, 16, 16])
        decall = sb("decall", [128, 16, 16])
        stbf2 = [sb(f"stbf{i}", [128, 1024], BF16) for i in range(2)]
        Cmk2 = [sb(f"Cmk{i}", [128, 2, 128], BF16) for i in range(2)]
        Bmk2 = [sb(f"Bmk{i}", [128, 256], BF16) for i in range(2)]

        pf = [ps(f"pf{i}", [128, 512]) for i in range(6)]
        pbk = [ps(f"pb{i}", [128, 1024], BF16) for i in range(2)]
        bank_ctr = [0]
        pool_n = [4]
        pb_ctr = [0]

        def bank():
            i = bank_ctr[0] % pool_n[0]
            bank_ctr[0] += 1
            return pf[i], ("pf", i)

        def bbank():
            i = pb_ctr[0] % 2
            pb_ctr[0] += 1
            return pbk[i], ("pb", i)

        ring_ctr = [0]

        def V(fn, r, w):
            return S.add("dve", fn, r, w)

        def A(fn, r, w):
            return S.add("act", fn, r, w)

        def G(fn, r, w):
            return S.add("pool", fn, r, w)

        def PE(fn, r, w):
            return S.add("pe", fn, r, w)

        def DM(q, fn, r, w):
            return S.add(q, fn, r, w, dma=True)

        DM("sp", lambda e: e.dma_start(out=cst[:], in_=cst_d), [], ["cst"])
        DM("sp", lambda e: e.dma_start(out=pcols[:], in_=pcols_d), [], ["pcols"])
        DM("sp", lambda e: e.dma_start(out=prow[:], in_=prow_d), [], ["prow"])
        DM("sp", lambda e: e.dma_start(out=bsT[:], in_=bsT_d), [], ["bsT"])
        DM("sp", lambda e: e.dma_start(out=gate[:], in_=gate_d), [], ["gate"])
        DM("sp", lambda e: e.dma_start(out=ctmp, in_=cT_d), [], ["ctmp"])
        DM("sp", lambda e: e.dma_start(out=wstmp, in_=wsT_d), [], ["wstmp"])
        DM("sp", lambda e: e.dma_start(out=cvT[:], in_=cvT_d), [], ["cvT"])
        DM("pool", lambda e: e.dma_start(out=colsel[:], in_=colsel_d), [], ["colsel"])
        V(lambda e: e.tensor_copy(out=identb[:], in_=ident), ["cst"], ["identb"])
        V(lambda e: e.tensor_copy(out=onesb[:], in_=ones), ["cst"], ["onesb"])
        V(lambda e: e.tensor_tensor(out=WT[:, 0:8, :], in0=wstmp[:, 0:8, :],
                                    in1=Lm.unsqueeze(1).to_broadcast([128, 8, 128]), op=ALU.mult),
          ["cst", "wstmp"], ["WT"])
        V(lambda e: e.tensor_tensor(out=WT[:, 8:16, :], in0=wstmp[:, 8:16, :],
                                    in1=LmB.unsqueeze(1).to_broadcast([128, 8, 128]), op=ALU.mult),
          ["cst", "wstmp"], ["WT"])
        A(lambda e: e.activation(out=scT[:], in_=ctmp, func=AF.Silu), ["ctmp"], ["scT"])
        V(lambda e: e.memset(ST[:], 0.0), [], ["ST"])
        V(lambda e: e.memset(STb[:], 0.0), [], ["STb"])
        V(lambda e: e.memset(tail[:], 0.0), [], ["tail"])
        A(lambda e: e.activation(out=Arow[:], in_=prow[:, PR_ALOG:PR_ALOG + 16], func=AF.Exp), ["prow"], ["Arow"])
        V(lambda e: e.tensor_scalar(out=Arow[:], in0=Arow[:], scalar1=-1.0, scalar2=None, op0=ALU.mult),
          ["Arow"], ["Arow"])
        V(lambda e: e.tensor_scalar(out=agin[:, 0:32], in0=pcols[:, PC_GIN:PC_GIN + 32], scalar1=ALPHA, scalar2=None,
                                    op0=ALU.mult), ["pcols"], ["agin"])
        V(lambda e: e.tensor_scalar(out=agin[:, 32:64], in0=pcols[:, PC_GMIX:PC_GMIX + 32], scalar1=ALPHA, scalar2=None,
                                    op0=ALU.mult), ["pcols"], ["agin"])

        def stream(wd, ncg, nkp, cgw, consumer, cgs=None):
            for cg in (range(ncg) if cgs is None else cgs):
                for kp in range(nkp):
                    slot = ring_ctr[0] % RING
                    ring_ctr[0] += 1
                    rk = ("ring", slot)
                    DM("pool", lambda e, slot=slot, cg=cg, kp=kp: e.dma_start(out=ring[slot][:, :, 0:cgw], in_=wd[cg, kp]),
                       [], [rk])
                    for kk in range(2):
                        consumer(cg, kp * 2 + kk, ring[slot][:, kk, 0:cgw], rk)

        modbanks = {}

        def mod_cons(cg, k, w, rk):
            if k == 0:
                modbanks[cg] = [bank() for _ in range(4)]
            for j in range(4):
                bk, bkey = modbanks[cg][j]
                PE(lambda e, bk=bk, j=j, k=k, w=w: e.matmul(bk[:, 0:17], lhsT=w[:, j * 128:(j + 1) * 128], rhs=scT[:, k, :],
                                                        start=(k == 0), stop=(k == 15)), [rk, "scT"], [bkey])
            if k == 15:
                for j in range(4):
                    bk, bkey = modbanks[cg][j]
                    ch = cg * 4 + j
                    V(lambda e, bk=bk, ch=ch: e.tensor_scalar(out=modT[:, ch, :], in0=bk[:, 0:17],
                                                             scalar1=pcols[:, PC_BMOD + ch:PC_BMOD + ch + 1], scalar2=None,
                                                             op0=ALU.add), [bkey, "pcols"], [("modT", ch // 16)])

        mod_left = list(range(24))

        def emit_mod(n):
            for _ in range(n):
                if not mod_left:
                    return
                stream(wmod_d, 24, 8, 512, mod_cons, cgs=[mod_left.pop(0)])

        emit_mod(8)

        def bc17(col0):
            return pcols[:, col0:col0 + 16].unsqueeze(2).to_broadcast([128, 16, 17])

        V(lambda e: e.tensor_scalar(out=ctmp, in0=modT[:, 16:32, :], scalar1=1.0, scalar2=None, op0=ALU.add),
          [("modT", 1)], ["ctmp"])
        V(lambda e: e.tensor_tensor(out=sclh[:], in0=ctmp, in1=bc17(PC_GIN), op=ALU.mult), ["ctmp", "pcols"], ["sclh"])
        V(lambda e: e.tensor_tensor(out=ctmp2, in0=ctmp, in1=bc17(PC_BIN), op=ALU.mult), ["ctmp", "pcols"], ["ctmp2"])
        V(lambda e: e.tensor_tensor(out=bish[:], in0=ctmp2, in1=modT[:, 0:16, :], op=ALU.add),
          ["ctmp2", ("modT", 0)], ["bish"])
        def ln_stats(src, TB, skey):
            b1, k1 = bank()
            b2, k2 = bank()
            for c in range(16):
                sqt = sq[c % 2]
                sqk = ("sq", c % 2)
                A(lambda e, c=c, sqt=sqt: e.activation(out=sqt[:, 0:TB], in_=src[:, c, 0:TB], func=AF.Square),
                  [(skey, c)], [sqk])
                PE(lambda e, c=c: e.matmul(b1[:, 0:TB], lhsT=ones, rhs=src[:, c, 0:TB], start=(c == 0), stop=(c == 15)),
                   [(skey, c), "cst"], [k1])
                PE(lambda e, c=c, sqt=sqt: e.matmul(b2[:, 0:TB], lhsT=onesb[:], rhs=sqt[:, 0:TB], start=(c == 0), stop=(c == 15)),
                   [sqk, "onesb"], [k2])
            V(lambda e: e.tensor_scalar(out=mean[:, 0:TB], in0=b1[:, 0:TB], scalar1=1.0 / D, scalar2=None, op0=ALU.mult),
              [k1], ["mean"])
            V(lambda e: e.tensor_tensor(out=vtmp[:, 0:TB], in0=mean[:, 0:TB], in1=mean[:, 0:TB], op=ALU.mult),
              ["mean"], ["vtmp"])
            V(lambda e: e.scalar_tensor_tensor(out=vtmp[:, 0:TB], in0=b2[:, 0:TB], scalar=1.0 / D, in1=vtmp[:, 0:TB],
                                               op0=ALU.mult, op1=ALU.subtract), [k2, "vtmp"], ["vtmp"])
            V(lambda e: e.tensor_scalar(out=vtmp[:, 0:TB], in0=vtmp[:, 0:TB], scalar1=EPS, scalar2=None, op0=ALU.add),
              ["vtmp"], ["vtmp"])
            A(lambda e: e.activation(out=vtmp[:, 0:TB], in_=vtmp[:, 0:TB], func=AF.Sqrt), ["vtmp"], ["vtmp"])
            V(lambda e: e.reciprocal(out=rstd[:, 0:TB], in_=vtmp[:, 0:TB]), ["vtmp"], ["rstd"])

        def ln_apply(src, TB, skey, fn):
            for cb_ in range(0, 16, 2):
                for i in range(2):
                    V(lambda e, i=i, c=cb_ + i: e.tensor_tensor(out=xn[i][:, 0:TB], in0=src[:, c, 0:TB], in1=mean[:, 0:TB],
                                                              op=ALU.subtract), [(skey, cb_ + i), "mean"], [("xn", i)])
                for i in range(2):
                    V(lambda e, i=i: e.tensor_tensor(out=xn[i][:, 0:TB], in0=xn[i][:, 0:TB], in1=rstd[:, 0:TB], op=ALU.mult),
                      [("xn", i), "rstd"], [("xn", i)])
                for i in range(2):
                    fn(cb_ + i, xn[i], ("xn", i))

        def mod_h(t, tk, c, TBp, TB, scl, bis, sk, bk_):
            if TBp > 0:
                A(lambda e: e.activation(out=hT[:, c, 0:TBp], in_=t[:, 0:TBp], func=AF.Identity,
                                         bias=bis[:, c, 0:1], scale=scl[:, c, 0:1]), [tk, sk, bk_], [("hT", c)])
            if TB > TBp:
                tv = t[:, TBp:TB].rearrange("p (b t) -> p b t", t=8)
                hv = hT[:, c, TBp:TB].rearrange("p (b t) -> p b t", t=8)
                V(lambda e: e.tensor_tensor(out=tv, in0=tv, in1=scl[:, c, 1:17].unsqueeze(2).to_broadcast([128, 16, 8]),
                                            op=ALU.mult), [tk, sk], [tk])
                V(lambda e: e.tensor_tensor(out=hv, in0=tv, in1=bis[:, c, 1:17].unsqueeze(2).to_broadcast([128, 16, 8]),
                                            op=ALU.add), [tk, bk_], [("hT", c)])

        def feat_linear(wd, ncg, K, act, akey, TB, evac):
            banks = {}

            def cons(cg, k, w, rk):
                if k == 0:
                    banks[cg] = [bank() for _ in range(4)]
                for j in range(4):
                    bk, bkey = banks[cg][j]
                    PE(lambda e, bk=bk, j=j, k=k, w=w: e.matmul(bk[:, 0:TB], lhsT=w[:, j * 128:(j + 1) * 128],
                                                            rhs=act[:, k, 0:TB], start=(k == 0), stop=(k == K - 1)),
                       [rk, (akey, k)], [bkey])
                if k == K - 1:
                    for j in range(4):
                        bk, bkey = banks[cg][j]
                        evac(cg * 4 + j, bk, bkey)

            stream(wd, ncg, K // 2, 512, cons)

        def run_block(src_d, col0, ntp, has_s, mode, hook=lambda: None):
            TBp = 128 * ntp
            TB = TBp + (128 if has_s else 0)
            pool_n[0] = 4
            ntl = ntp + (1 if has_s else 0)
            full = mode == "full"
            for c in range(16):
                DM("sp", lambda e, c=c: e.dma_start(out=xA[:, c, 0:TB], in_=src_d[c * 128:(c + 1) * 128, col0:col0 + TB]),
                   [], [("xA", c)])
            ln_stats(xA, TB, "xA")
            def fnA(c, t, tk):
                if full:
                    A(lambda e, c=c, t=t: e.activation(out=xA[:, c, 0:TB], in_=t[:, 0:TB], func=AF.Identity,
                                                       scale=agin[:, c:c + 1], bias=agin[:, 16 + c:17 + c]),
                      [tk, "agin"], [("xA", c)])
                mod_h(t, tk, c, TBp, TB, sclh, bish, "sclh", "bish")

            ln_apply(xA, TB, "xA", fnA)
            hook()

            dtb, dtk = pf[4], ("pf", 4)

            def conv_chunk(ch, j, bk, bkey):
                par = (ch // 4) % 2
                stg = stg2[par]
                sstg = sstg2[par]
                j2 = j
                j = (par, j2)
                return conv_chunk_(ch, j2, j, stg, sstg, bk, bkey)

            def run_rr(gl):
                act_ = list(gl)
                while act_:
                    for g_ in list(act_):
                        try:
                            next(g_)
                        except StopIteration:
                            act_.remove(g_)

            def conv_chunk_(ch, jj, j, stg, sstg, bk, bkey):
                cacc = cacc2[jj % 2]
                ck = ("cacc", jj % 2)
                cw = pcols[:, PC_CW + ch * 4:PC_CW + ch * 4 + 4]
                cb = pcols[:, PC_CB + ch:PC_CB + ch + 1]
                if TBp > 0:
                    A(lambda e: e.activation(out=cacc[:, 0:TBp], in_=bk[:, 0:TBp], func=AF.Identity, scale=cw[:, 3:4], bias=cb),
                      [bkey, "pcols"], [ck])
                    A(lambda e: e.activation(out=stg[:, jj, 3:3 + TBp], in_=bk[:, 0:TBp], func=AF.Identity),
                      [bkey], [("stg", j)])
                    yield
                    A(lambda e: e.activation(out=stg[:, jj, 0:3], in_=tail[:, ch, :], func=AF.Identity), [("tail", ch)], [("stg", j)])
                    yield
                    for k in range(0, 3):
                        V(lambda e, k=k: e.scalar_tensor_tensor(out=cacc[:, 0:TBp], in0=stg[:, jj, k:k + TBp],
                                                                scalar=cw[:, k:k + 1], in1=cacc[:, 0:TBp],
                                                                op0=ALU.mult, op1=ALU.add), [("stg", j), ck], [ck])
                        yield
                    A(lambda e: e.activation(out=xc[:, ch, 0:TBp], in_=cacc[:, 0:TBp], func=AF.Silu), [ck], [("xc", ch)])
                    yield
                    A(lambda e: e.activation(out=tail[:, ch, :], in_=stg[:, jj, TBp:TBp + 3], func=AF.Identity), [("stg", j)], [("tail", ch)])
                if has_s:
                    A(lambda e: e.activation(out=sstg[:, jj, :, 3:11], in_=bk[:, TBp:TB].rearrange("p (b t) -> p b t", t=8),
                                             func=AF.Identity), [bkey], [("sstg", j)])
                    A(lambda e: e.activation(out=sstg[:, jj, :, 0:3], in_=cvT[:, ch, :, :], func=AF.Identity), ["cvT"], [("sstg", j)])
                    cv = cacc[:, 0:128].rearrange("p (b t) -> p b t", t=8)
                    A(lambda e: e.activation(out=cv, in_=bk[:, TBp:TB].rearrange("p (b t) -> p b t", t=8), func=AF.Identity,
                                             scale=cw[:, 3:4], bias=cb), [bkey, "pcols", ("xc", ch)], [ck])
                    yield
                    for k in range(0, 3):
                        V(lambda e, k=k: e.scalar_tensor_tensor(out=cv, in0=sstg[:, jj, :, k:k + 8], scalar=cw[:, k:k + 1],
                                                                in1=cv, op0=ALU.mult, op1=ALU.add),
                          [("sstg", j), ck], [ck])
                        yield
                    A(lambda e: e.activation(out=xc[:, ch, TBp:TB].rearrange("p (b t) -> p b t", t=8), in_=cv, func=AF.Silu),
                      [ck], [("xc", ch)])
                    A(lambda e: e.activation(out=convs[:, ch, :, :], in_=sstg[:, jj, :, 8:11], func=AF.Identity), [("sstg", j)], ["convs"])

            xbanks = {}

            def xbc_cons(cg, k, w, rk, base):
                g = base + cg
                if k == 0:
                    xbanks[g] = [bank() for _ in range(4)]
                for j in range(4):
                    bk, bkey = xbanks[g][j]
                    PE(lambda e, bk=bk, j=j, k=k, w=w: e.matmul(bk[:, 0:TB], lhsT=w[:, j * 128:(j + 1) * 128],
                                                            rhs=hT[:, k, 0:TB], start=(k == 0), stop=(k == 15)),
                       [rk, ("hT", k)], [bkey])
                if g == 2:
                    for t in range(ntl):
                        PE(lambda e, t=t, k=k, w=w: e.matmul(dtb[:, t * 16:(t + 1) * 16], lhsT=hT[:, k, t * 128:(t + 1) * 128],
                                                         rhs=w[:, 512:528], start=(k == 0 and t == 0), stop=(k == 15),
                                                         skip_group_check=True),
                           [rk, ("hT", k)], [dtk])
                if k == 15:
                    for jp in (0, 2):
                        run_rr([conv_chunk(g * 4 + j, j, xbanks[g][j][0], xbanks[g][j][1]) for j in (jp, jp + 1)])
                    hook()

            stream(wina_d[0:2], 2, 8, 512, lambda cg, k, w, rk: xbc_cons(cg, k, w, rk, 0))
            stream(winb_d, 1, 8, 528, lambda cg, k, w, rk: xbc_cons(cg, k, w, rk, 2))

            def tok_linear(gbase, evac):
                tb = {}

                def cons(cg, k, w, rk):
                    if k == 0:
                        tb[cg] = [bank() for _ in range(ntl)]
                    for t in range(ntl):
                        bk, bkey = tb[cg][t]
                        PE(lambda e, bk=bk, t=t, k=k, w=w: e.matmul(bk[:, 0:512], lhsT=hT[:, k, t * 128:(t + 1) * 128], rhs=w,
                                                                start=(k == 0), stop=(k == 15)), [rk, ("hT", k)], [bkey])
                    if k == 15:
                        for t in range(ntl):
                            bk, bkey = tb[cg][t]
                            evac(cg, t, bk, bkey)

                stream(wina_d[gbase:gbase + 2], 2, 8, 512, cons)

            if full:
                tok_linear(4, lambda cg, t, bk, bkey: A(
                    lambda e: e.activation(out=gv[:, t, cg * 512:(cg + 1) * 512], in_=bk[:, 0:512], func=AF.Gelu_apprx_tanh),
                    [bkey], [("gv", t)]))

            def ssd_tile(t, is_s):
                c0 = t * 128
                sm = smt[t]

                def small(i):
                    return sm[:, i, :]
                Lmask = LmB if is_s else Lm
                Omask = BlkO if is_s else ones
                pb, pbkey = bbank()
                for ch in range(8):
                    PE(lambda e, ch=ch: e.transpose(out=pb[:, ch * 128:(ch + 1) * 128], in_=xc[:, ch, c0:c0 + 128], identity=identb[:]),
                       [("xc", ch), "identb"], [pbkey])
                A(lambda e: e.activation(out=x_tm[:, t, :], in_=pb[:, :], func=AF.Identity), [pbkey], [("x_tm", t)])
                yield "p1"
                pb2, pb2key = bbank()
                for g in range(2):
                    PE(lambda e, g=g: e.transpose(out=pb2[:, g * 128:(g + 1) * 128], in_=xc[:, 8 + g, c0:c0 + 128], identity=identb[:]),
                       [("xc", 8 + g), "identb"], [pb2key])
                A(lambda e: e.activation(out=B_tm[:, t, :], in_=pb2[:, 0:256], func=AF.Identity), [pb2key], [("B_tm", t)])
                yield "p1"
                dtr, mx, ab, ex, dtv, av, acs, dd, dend, cdec, eac, dtx = [small(i) for i in range(12)]
                V(lambda e: e.tensor_tensor(out=dtr, in0=dtb[:, t * 16:(t + 1) * 16], in1=prow[:, PR_DTB:PR_DTB + 16], op=ALU.add),
                  [dtk, "prow"], [("sm", t)])
                V(lambda e: e.tensor_scalar(out=mx, in0=dtr, scalar1=0.0, scalar2=None, op0=ALU.max), [("sm", t)], [("sm", t)])
                V(lambda e: e.tensor_scalar(out=ab, in0=dtr, scalar1=-1.0, scalar2=None, op0=ALU.mult), [("sm", t)], [("sm", t)])
                V(lambda e: e.tensor_tensor(out=ab, in0=ab, in1=dtr, op=ALU.max), [("sm", t)], [("sm", t)])
                yield "p1"
                A(lambda e: e.activation(out=ex, in_=ab, func=AF.Exp, scale=-1.0), [("sm", t)], [("sm", t)])
                yield "p1"
                A(lambda e: e.activation(out=ex, in_=ex, func=AF.Ln, bias=1.0), [("sm", t)], [("sm", t)])
                yield "p1"
                V(lambda e: e.tensor_tensor(out=dtv, in0=mx, in1=ex, op=ALU.add), [("sm", t)], [("sm", t)])
                V(lambda e: e.tensor_tensor(out=av, in0=dtv, in1=Arow[:], op=ALU.mult), [("sm", t), "Arow"], [("sm", t)])
                yield "p1"
                pa, pak = bank()
                PE(lambda e: e.matmul(pa[:, 0:16], lhsT=Lmask, rhs=av, start=True, stop=True), [("sm", t), "cst"], [pak])
                PE(lambda e: e.matmul(pa[:, 16:32], lhsT=Omask, rhs=av, start=True, stop=True), [("sm", t), "cst"], [pak])
                yield "p1"
                V(lambda e: e.tensor_copy(out=acs, in_=pa[:, 0:16]), [pak], [("sm", t)])
                V(lambda e: e.tensor_tensor(out=dd, in0=pa[:, 16:32], in1=acs, op=ALU.subtract), [pak, ("sm", t)], [("sm", t)])
                yield "p1"
                A(lambda e: e.activation(out=dend, in_=dd, func=AF.Exp), [("sm", t)], [("sm", t)])
                A(lambda e: e.activation(out=cdec, in_=pa[:, 16:32], func=AF.Exp), [pak], [("sm", t)])
                A(lambda e: e.activation(out=eac, in_=acs, func=AF.Exp), [("sm", t)], [("sm", t)])
                yield "p1"
                V(lambda e: e.tensor_tensor(out=dtx, in0=dtv, in1=dend, op=ALU.mult), [("sm", t)], [("sm", t)])

                yield "end1"

                def b64(v):
                    return v.unsqueeze(2).to_broadcast([128, 16, 64])

                x3 = x_tm[:, t, :].rearrange("p (h d) -> p h d", d=64)
                V(lambda e: e.tensor_tensor(out=xdd[:].rearrange("p (h d) -> p h d", d=64), in0=x3, in1=b64(dtx), op=ALU.mult),
                  [("x_tm", t), ("sm", t)], ["xdd"])
                if full:
                    V(lambda e: e.tensor_tensor(out=xdt[:].rearrange("p (h d) -> p h d", d=64), in0=x3, in1=b64(dtv), op=ALU.mult),
                      [("x_tm", t), ("sm", t)], ["xdt"])
                    V(lambda e: e.tensor_tensor(out=xDs[:].rearrange("p (h d) -> p h d", d=64), in0=x3,
                                                in1=b64(prow[:, PR_DSK:PR_DSK + 16]), op=ALU.mult),
                      [("x_tm", t), "prow"], ["xDs"])
                    pc, pck = bank()
                    for g in range(2):
                        PE(lambda e, g=g: e.matmul(pc[:, g * 128:(g + 1) * 128], lhsT=xc[:, 8 + g, c0:c0 + 128],
                                                   rhs=xc[:, 10 + g, c0:c0 + 128], start=True, stop=True),
                           [("xc", 8 + g), ("xc", 10 + g)], [pck])
                    V(lambda e: e.tensor_tensor(out=CBm[:], in0=pc[:, 0:256].rearrange("p (g i) -> p g i", i=128),
                                                in1=Lmask.unsqueeze(1).to_broadcast([128, 2, 128]), op=ALU.mult),
                      [pck, "cst"], ["CBm"])
                    for qa in (0, 2):
                        qs = (qa, qa + 1)
                        for q in qs:
                            V(lambda e, q=q: e.tensor_tensor(out=lh[q % 2][:], in0=Um.unsqueeze(1).to_broadcast([128, 4, 128]),
                                                             in1=av[:, 4 * q:4 * q + 4].unsqueeze(2).to_broadcast([128, 4, 128]),
                                                             op=ALU.mult), ["cst", ("sm", t)], [("lh", q % 2)])
                        pgs = {}
                        for q in qs:
                            pg, pgk = bank()
                            pgs[q] = (pg, pgk)
                            for hh in range(4):
                                PE(lambda e, hh=hh, q=q, pg=pg: e.matmul(pg[:, hh * 128:(hh + 1) * 128], lhsT=lh[q % 2][:, hh, :], rhs=Lm,
                                                                        start=True, stop=True), [("lh", q % 2), "cst"], [pgk])
                        for q in qs:
                            pg, pgk = pgs[q]
                            A(lambda e, q=q, pg=pg: e.activation(out=Ee[q % 2][:].rearrange("p a b -> p (a b)"), in_=pg[:, :], func=AF.Exp),
                              [pgk], [("Ee", q % 2)])
                        for q in qs:
                            V(lambda e, q=q: e.tensor_tensor(out=Mm[:, 4 * q:4 * q + 4, :], in0=Ee[q % 2][:],
                                                             in1=CBm[:, q // 2, :].unsqueeze(1).to_broadcast([128, 4, 128]),
                                                             op=ALU.mult), [("Ee", q % 2), "CBm"], [("Mm", q)])
                    if not is_s:
                        for g in range(2):
                            po, pok = bank()
                            PE(lambda e, g=g, po=po: e.matmul(po[:, :], lhsT=xc[:, 10 + g, c0:c0 + 128], rhs=STb[:, g * 512:(g + 1) * 512],
                                                              start=True, stop=True), [("xc", 10 + g), "STb"], [pok])
                            V(lambda e, g=g, po=po: e.tensor_tensor(
                                out=yy[:, t, g * 512:(g + 1) * 512].rearrange("p (h d) -> p h d", d=64),
                                in0=po[:, :].rearrange("p (h d) -> p h d", d=64),
                                in1=eac[:, 8 * g:8 * g + 8].unsqueeze(2).to_broadcast([128, 8, 64]), op=ALU.mult),
                              [pok, ("sm", t)], [("yy", t)])
                    else:
                        sample_states(t, c0, av, eac, dtx)
                    for g in range(2):
                        py, pyk = bank()
                        PE(lambda e, g=g, py=py: e.matmul(py[:, :], lhsT=identb[:], rhs=xDs[:, g * 512:(g + 1) * 512],
                                                          start=True, stop=False), ["identb", "xDs"], [pyk])
                        for h8 in range(8):
                            h = g * 8 + h8
                            PE(lambda e, h=h, h8=h8, py=py: e.matmul(py[:, h8 * 64:(h8 + 1) * 64], lhsT=Mm[:, h, :],
                                                                     rhs=xdt[:, h * 64:(h + 1) * 64], start=False, stop=(h8 == 7)),
                               [("Mm", h // 4), "xdt"], [pyk])
                        V(lambda e, g=g, py=py: e.tensor_tensor(out=yy[:, t, g * 512:(g + 1) * 512], in0=py[:, :],
                                                                in1=yy[:, t, g * 512:(g + 1) * 512], op=ALU.add),
                          [pyk, ("yy", t)], [("yy", t)])
                if not is_s:
                    V(lambda e: e.tensor_tensor(out=ST[:].rearrange("p (h d) -> p h d", d=64),
                                                in0=ST[:].rearrange("p (h d) -> p h d", d=64), in1=b64(cdec), op=ALU.mult),
                      ["ST", ("sm", t)], ["ST"])
                    for g in range(2):
                        pt, ptk = bank()
                        PE(lambda e, g=g, pt=pt: e.matmul(pt[:, :], lhsT=B_tm[:, t, g * 128:(g + 1) * 128], rhs=xdd[:, g * 512:(g + 1) * 512],
                                                          start=True, stop=True), [("B_tm", t), "xdd"], [ptk])
                        V(lambda e, g=g, pt=pt: e.tensor_tensor(out=ST[:, g * 512:(g + 1) * 512], in0=pt[:, :],
                                                                in1=ST[:, g * 512:(g + 1) * 512], op=ALU.add), [ptk, "ST"], ["ST"])
                    A(lambda e: e.activation(out=STb[:], in_=ST[:], func=AF.Identity), ["ST"], ["STb"])

            def sample_states(t, c0, av, eac, dtx):
                V(lambda e: e.tensor_tensor(out=ablk[:], in0=av.unsqueeze(1).to_broadcast([128, 16, 16]),
                                            in1=Bsel.unsqueeze(2).to_broadcast([128, 16, 16]), op=ALU.mult),
                  [("sm", t), "cst"], ["ablk"])
                pd, pdk = bank()
                PE(lambda e: e.matmul(pd[:, 0:256], lhsT=ones, rhs=ablk[:].rearrange("p a b -> p (a b)"), start=True, stop=True),
                   ["ablk", "cst"], [pdk])
                A(lambda e: e.activation(out=decall[:].rearrange("p a b -> p (a b)"), in_=pd[:, 0:256], func=AF.Exp),
                  [pdk], ["decall"])
                po = [(pf[4], ("pf", 4)), (pf[5], ("pf", 5))]
                slots = [stin[0], stin[1], snew]

                def names(bb):
                    return (slots[bb % 3], ("stin", bb % 3), stbf2[bb % 2], ("stbf", bb % 2), Cmk2[bb % 2], ("Cmk", bb % 2),
                            Bmk2[bb % 2], ("Bmk", bb % 2))

                def stage1(bb):
                    si, sk, sbf, sbk, cm_, cmk, bm_, bmk = names(bb)
                    DM("sp", lambda e: e.dma_start(out=si, in_=stT_d[bb]), [], [sk])
                    A(lambda e: e.activation(out=sbf[:], in_=si, func=AF.Identity), [sk], [sbk])
                    V(lambda e: e.tensor_tensor(out=cm_[:], in0=xc[:, 10:12, c0:c0 + 128],
                                                in1=colsel[:, bb, :].unsqueeze(1).to_broadcast([128, 2, 128]), op=ALU.mult),
                      [("xc", 10), ("xc", 11), "colsel"], [cmk])
                    V(lambda e: e.tensor_scalar(out=bm_[:], in0=B_tm[:, t, :], scalar1=Bsel[:, bb:bb + 1], scalar2=None,
                                                op0=ALU.mult), [("B_tm", t), "cst"], [bmk])

                def stage2(bb):
                    si, sk, sbf, sbk, cm_, cmk, bm_, bmk = names(bb)
                    for g in range(2):
                        PE(lambda e, g=g: e.matmul(po[g][0][:, :], lhsT=cm_[:, g, :], rhs=sbf[:, g * 512:(g + 1) * 512],
                                                   start=(bb == 0), stop=(bb == 15)), [cmk, sbk], [po[g][1]])
                    V(lambda e: e.tensor_tensor(out=si.rearrange("p (h d) -> p h d", d=64),
                                                in0=si.rearrange("p (h d) -> p h d", d=64),
                                                in1=decall[:, bb, :].unsqueeze(2).to_broadcast([128, 16, 64]), op=ALU.mult),
                      [sk, "decall"], [sk])
                    for g in range(2):
                        pt, ptk = bank()
                        PE(lambda e, g=g, pt=pt: e.matmul(pt[:, :], lhsT=bm_[:, g * 128:(g + 1) * 128], rhs=xdd[:, g * 512:(g + 1) * 512],
                                                          start=True, stop=True), [bmk, "xdd"], [ptk])
                        V(lambda e, g=g, pt=pt: e.tensor_tensor(out=si[:, g * 512:(g + 1) * 512], in0=pt[:, :],
                                                                in1=si[:, g * 512:(g + 1) * 512], op=ALU.add), [ptk, sk], [sk])
                    DM("sp", lambda e: e.dma_start(out=ssmsT_d[bb], in_=si), [sk], [])

                for bb in range(17):
                    if bb < 16:
                        stage1(bb)
                    if bb > 0:
                        stage2(bb - 1)
                for g in range(2):
                    V(lambda e, g=g: e.tensor_tensor(out=yy[:, t, g * 512:(g + 1) * 512].rearrange("p (h d) -> p h d", d=64),
                                                     in0=po[g][0][:, :].rearrange("p (h d) -> p h d", d=64),
                                                     in1=eac[:, 8 * g:8 * g + 8].unsqueeze(2).to_broadcast([128, 8, 64]), op=ALU.mult),
                      [po[g][1], ("sm", t)], [("yy", t)])

            gens = [ssd_tile(t, has_s and t == ntl - 1) for t in range(ntl)]
            active = list(gens)
            while active:
                for g_ in list(active):
                    if next(g_) == "end1":
                        active.remove(g_)
            for g_ in gens:
                for _ in g_:
                    pass
                hook()

            if not full:
                return
            if has_s:
                S.barrier(lambda e: e.memset(dummy[:], 0.0))

            def z_evac(cg, t, bk, bkey):
                A(lambda e: e.activation(out=gut[:], in_=bk[:, 0:512], func=AF.Silu), [bkey], ["gut"])
                V(lambda e: e.tensor_tensor(out=yy[:, t, cg * 512:(cg + 1) * 512], in0=yy[:, t, cg * 512:(cg + 1) * 512], in1=gut[:],
                                            op=ALU.mult), ["gut", ("yy", t)], [("yy", t)])

            tok_linear(2, z_evac)

            for t in range(ntl):
                for g in range(2):
                    A(lambda e, t=t, g=g: e.activation(out=gut[:], in_=yy[:, t, g * 512:(g + 1) * 512], func=AF.Square,
                                                       accum_out=ss2t[:, t, g:g + 1]), [("yy", t)], ["gut", ("ss2", t)])
            for t in range(ntl):
                V(lambda e, t=t: e.tensor_scalar(out=ss2t[:, t, 2:4], in0=ss2t[:, t, 0:2], scalar1=1.0 / 512, scalar2=EPS,
                                                 op0=ALU.mult, op1=ALU.add), [("ss2", t)], [("ss2", t)])
            for t in range(ntl):
                A(lambda e, t=t: e.activation(out=ss2t[:, t, 2:4], in_=ss2t[:, t, 2:4], func=AF.Sqrt), [("ss2", t)], [("ss2", t)])
            for t in range(ntl):
                V(lambda e, t=t: e.reciprocal(out=ss2t[:, t, 2:4], in_=ss2t[:, t, 2:4]), [("ss2", t)], [("ss2", t)])
            for t in range(ntl):
                c0 = t * 128
                for g in range(2):
                    V(lambda e, t=t, g=g: e.tensor_scalar(out=ysb[:, g * 512:(g + 1) * 512], in0=yy[:, t, g * 512:(g + 1) * 512],
                                                          scalar1=ss2t[:, t, 2 + g:3 + g], scalar2=None, op0=ALU.mult),
                      [("yy", t), ("ss2", t)], ["ysb"])
                pb, pbkey = bbank()
                for ch in range(8):
                    PE(lambda e, ch=ch, pb=pb: e.transpose(out=pb[:, ch * 128:(ch + 1) * 128], in_=ysb[:, ch * 128:(ch + 1) * 128],
                                                          identity=identb[:]), ["ysb", "identb"], [pbkey])
                V(lambda e, c0=c0, pb=pb: e.tensor_tensor(out=mixT[:, 0:8, c0:c0 + 128], in0=pb[:, :].rearrange("p (c t) -> p c t", t=128),
                                                          in1=pcols[:, PC_NG:PC_NG + 8].unsqueeze(2).to_broadcast([128, 8, 128]), op=ALU.mult),
                  [pbkey, "pcols"], [("mixT", c) for c in range(8)])

            for t in range(ntl):
                for g in range(2):
                    V(lambda e, t=t, g=g: e.bn_stats(out=bstt[:, t, g * 6:(g + 1) * 6], in_=gv[:, t, g * 512:(g + 1) * 512]),
                      [("gv", t)], [("bst", t)])
            for t in range(ntl):
                V(lambda e, t=t: e.bn_aggr(out=bmvt[:, t, :], in_=bstt[:, t, :].rearrange("p (a b) -> p a b", b=6)),
                  [("bst", t)], [("bmv", t)])
            for t in range(ntl):
                V(lambda e, t=t: e.tensor_scalar(out=bmvt[:, t, 1:2], in0=bmvt[:, t, 1:2], scalar1=EPS, scalar2=None, op0=ALU.add),
                  [("bmv", t)], [("bmv", t)])
            for t in range(ntl):
                A(lambda e, t=t: e.activation(out=bmvt[:, t, 1:2], in_=bmvt[:, t, 1:2], func=AF.Sqrt), [("bmv", t)], [("bmv", t)])
            for t in range(ntl):
                V(lambda e, t=t: e.reciprocal(out=bmvt[:, t, 1:2], in_=bmvt[:, t, 1:2]), [("bmv", t)], [("bmv", t)])
            for t in range(ntl):
                is_s = has_s and t == ntl - 1
                V(lambda e, t=t: e.tensor_scalar(out=gv[:, t, :], in0=gv[:, t, :], scalar1=bmvt[:, t, 0:1], scalar2=bmvt[:, t, 1:2],
                                                 op0=ALU.subtract, op1=ALU.mult), [("gv", t), ("bmv", t)], [("gv", t)])
                V(lambda e, t=t: e.tensor_tensor(out=gv[:, t, :], in0=gv[:, t, :], in1=prow[:, PR_GG:PR_GG + 1024], op=ALU.mult),
                  [("gv", t), "prow"], [("gv", t)])
                V(lambda e, t=t: e.tensor_tensor(out=gv[:, t, :], in0=gv[:, t, :], in1=prow[:, PR_GB:PR_GB + 1024], op=ALU.add),
                  [("gv", t), "prow"], [("gv", t)])
                if is_s:
                    DM("sp", lambda e, t=t: e.dma_start(out=vs_d, in_=gv[:, t, :]), [("gv", t)], [])
                A(lambda e, t=t: e.activation(out=vnb[:], in_=gv[:, t, :], func=AF.Identity), [("gv", t)], ["vnb"])
                wb = 8 if is_s else 0
                for g in range(2):
                    pm, pmk = bank()
                    for h4 in range(4):
                        h = g * 4 + h4
                        PE(lambda e, h=h, h4=h4, pm=pm, wb=wb: e.matmul(pm[:, h4 * 128:(h4 + 1) * 128], lhsT=WT[:, wb + h, :],
                                                                       rhs=vnb[:, h * 128:(h + 1) * 128], start=True, stop=True),
                           ["WT", "vnb"], [pmk])
                    V(lambda e, t=t, g=g, pm=pm, wb=wb: e.tensor_tensor(
                        out=gv[:, t, g * 512:(g + 1) * 512].rearrange("p (h d) -> p h d", d=128),
                        in0=pm[:, :].rearrange("p (h d) -> p h d", d=128),
                        in1=bsT[:, wb + g * 4:wb + g * 4 + 4].unsqueeze(2).to_broadcast([128, 4, 128]), op=ALU.add),
                      [pmk, "bsT", ("gv", t)], [("gv", t)])

            def u_evac(cg, t, bk, bkey):
                A(lambda e: e.activation(out=gut[:], in_=bk[:, 0:512], func=AF.Gelu_apprx_tanh), [bkey], ["gut"])
                V(lambda e: e.tensor_tensor(out=ysb[:, cg * 512:(cg + 1) * 512], in0=gut[:], in1=gv[:, t, cg * 512:(cg + 1) * 512],
                                            op=ALU.mult), ["gut", ("gv", t)], ["ysb"])
                pb, pbkey = bbank()
                for c4 in range(4):
                    ch = cg * 4 + c4
                    PE(lambda e, ch=ch, c4=c4, pb=pb: e.transpose(out=pb[:, c4 * 128:(c4 + 1) * 128], in_=ysb[:, ch * 128:(ch + 1) * 128],
                                                                  identity=identb[:]), ["ysb", "identb"], [pbkey])
                c0 = t * 128
                A(lambda e, pb=pb: e.activation(out=mixT[:, 8 + cg * 4:12 + cg * 4, c0:c0 + 128],
                                                in_=pb[:, 0:512].rearrange("p (c t) -> p c t", t=128), func=AF.Identity),
                  [pbkey], [("mixT", 8 + cg * 4 + i) for i in range(4)])

            tok_linear(6, u_evac)

            S.barrier(lambda e: e.memset(dummy[:], 0.0))
            pool_n[0] = 6
            def res_evac(Gt, gk):
                def ev(ch, bk, bkey):
                    if TBp > 0:
                        V(lambda e: e.scalar_tensor_tensor(out=xA[:, ch, 0:TBp], in0=bk[:, 0:TBp], scalar=Gt[:, ch, 0:1],
                                                           in1=xA[:, ch, 0:TBp], op0=ALU.mult, op1=ALU.add),
                          [bkey, gk, ("xA", ch)], [("xA", ch)])
                    if has_s:
                        r_ = rl[ch % 2]
                        rk_ = ("rl", ch % 2)
                        V(lambda e: e.tensor_tensor(out=r_[:, 0:128].rearrange("p (b t) -> p b t", t=8),
                                                    in0=bk[:, TBp:TB].rearrange("p (b t) -> p b t", t=8),
                                                    in1=Gt[:, ch, 1:17].unsqueeze(2).to_broadcast([128, 16, 8]), op=ALU.mult),
                          [bkey, gk], [rk_])
                        V(lambda e: e.tensor_tensor(out=xA[:, ch, TBp:TB], in0=r_[:, 0:128], in1=xA[:, ch, TBp:TB], op=ALU.add),
                          [rk_, ("xA", ch)], [("xA", ch)])
                return ev

            feat_linear(wout_d, 4, 16, mixT, "mixT", TB, res_evac(Gm, "Gm"))

            ln_stats(xA, TB, "xA")
            def fnD(c, t, tk):
                A(lambda e, c=c, t=t: e.activation(out=xA[:, c, 0:TB], in_=t[:, 0:TB], func=AF.Identity,
                                                   scale=agin[:, 32 + c:33 + c], bias=agin[:, 48 + c:49 + c]),
                  [tk, "agin"], [("xA", c)])
                mod_h(t, tk, c, TBp, TB, sclh2, bish2, "sclh2", "bish2")

            ln_apply(xA, TB, "xA", fnD)

            def ff1_evac(ch, bk, bkey):
                r_ = rl[ch % 2]
                rk_ = ("rl", ch % 2)
                A(lambda e: e.activation(out=r_[:, 0:TB], in_=bk[:, 0:TB], func=AF.Relu), [bkey], [rk_])
                V(lambda e: e.tensor_tensor(out=hid[:, ch, 0:TB], in0=r_[:, 0:TB], in1=r_[:, 0:TB], op=ALU.mult),
                  [rk_], [("hid", ch)])

            feat_linear(wff1_d, 16, 16, hT, "hT", TB, ff1_evac)
            feat_linear(wff2_d, 4, 64, hid, "hid", TB, res_evac(Gf, "Gf"))
            ln_stats(xA, TB, "xA")
            def fnG(c, t, tk):
                A(lambda e, c=c, t=t: e.activation(out=xA[:, c, 0:TB], in_=t[:, 0:TB], func=AF.Identity,
                                                   scale=pcols[:, PC_GFFN + c:PC_GFFN + c + 1], bias=pcols[:, PC_BFFN + c:PC_BFFN + c + 1]),
                  [tk, "pcols"], [("xA", c)])
                DM("sp", lambda e, c=c: e.dma_start(out=yT_d[c * 128:(c + 1) * 128, col0:col0 + TB], in_=xA[:, c, 0:TB]),
                   [("xA", c)], [])

            ln_apply(xA, TB, "xA", fnG)

        t0 = 0
        while t0 < 8:
            n = min(nt, 8 - t0)
            run_block(xpT_d, t0 * 128, n, False, "prefix", hook=lambda: emit_mod(1))
            t0 += n
        V(lambda e: e.tensor_scalar(out=ST[:], in0=ST[:], scalar1=gate[:, 0:1], scalar2=None, op0=ALU.mult), ["ST", "gate"], ["ST"])
        A(lambda e: e.activation(out=STb[:], in_=ST[:], func=AF.Identity), ["ST"], ["STb"])
        V(lambda e: e.tensor_scalar(out=tail[:].rearrange("p a b -> p (a b)"), in0=tail[:].rearrange("p a b -> p (a b)"),
                                    scalar1=gate[:, 0:1], scalar2=None, op0=ALU.mult),
          [("tail", c) for c in range(12)] + ["gate"], [("tail", c) for c in range(12)])
        emit_mod(24)
        V(lambda e: e.tensor_scalar(out=Gm[:], in0=modT[:, 32:48, :], scalar1=1.0, scalar2=None, op0=ALU.add),
          [("modT", 2)], ["Gm"])
        V(lambda e: e.tensor_scalar(out=ctmp, in0=modT[:, 64:80, :], scalar1=1.0, scalar2=None, op0=ALU.add),
          [("modT", 4), "sclh", "ctmp2"], ["ctmp"])
        V(lambda e: e.tensor_tensor(out=sclh2[:], in0=ctmp, in1=bc17(PC_GMIX), op=ALU.mult), ["ctmp", "pcols"], ["sclh2"])
        V(lambda e: e.tensor_tensor(out=ctmp2, in0=ctmp, in1=bc17(PC_BMIX), op=ALU.mult), ["ctmp", "pcols"], ["ctmp2"])
        V(lambda e: e.tensor_tensor(out=bish2[:], in0=ctmp2, in1=modT[:, 48:64, :], op=ALU.add),
          ["ctmp2", ("modT", 3)], ["bish2"])
        V(lambda e: e.tensor_scalar(out=Gf[:], in0=modT[:, 80:96, :], scalar1=1.0, scalar2=None, op0=ALU.add),
          [("modT", 5)], ["Gf"])

        S.barrier(lambda e: e.memset(dummy[:], 0.0))
        tiles = [("P", i) for i in range(8)] + [("S", 0)]
        i = 0
        while i < 9:
            grp = tiles[i:i + nt]
            ntp = sum(1 for g in grp if g[0] == "P")
            has_s = any(g[0] == "S" for g in grp)
            run_block(xT_d, i * 128, ntp, has_s, "full")
            if ntp > 0 and grp[ntp - 1] == ("P", 7):
                DM("sp", lambda e: e.dma_start(out=ssmT_d, in_=ST[:]), ["ST"], [])
                DM("sp", lambda e: e.dma_start(out=convT_d, in_=tail[:]), [("tail", c) for c in range(12)], [])
            i += nt
        DM("sp", lambda e: e.dma_start(out=convsT_d, in_=convs[:]), ["convs"], [])
        allq = [op for q in ["sp", "pool"] for op in S.ops[q] if op.is_dma]
        fin = S.add("sp", lambda e: e.engine_nop() if hasattr(e, "engine_nop") else e.sem_inc(sems["sp"], 0))
        last = {}
        for op in allq:
            last[op.sem] = op
        fin.deps.extend(last.values())
        S.finalize()
        with nc.Block() as block:
            S.emit(block, sems, dsems)
    return nc


def _tile_w(w, groups, cgw):
    K = w.shape[0]
    out = np.empty((len(groups), K // 256, 128, 2, cgw), np.float32)
    for gi, c0 in enumerate(groups):
        blk = w[:, c0:c0 + cgw].reshape(K // 256, 2, 128, cgw)
        out[gi] = blk.transpose(0, 2, 1, 3)
    return out


_CACHE = {}


def kernel(x_prompt, x_sample, state_ssm, state_conv, c_prompt, c_sample, ln_in_g, ln_in_b,
           w_mod, b_mod, w_in, conv_w, conv_b, dt_bias, a_log, d_skip, ssd_norm_g, gm_ln_g, gm_ln_b,
           gm_w_s, gm_b_s, w_out, ln_mix_g, ln_mix_b, w_ff1, w_ff2, ln_ffn_g, ln_ffn_b):
    f = np.float32
    a = lambda v: np.ascontiguousarray(np.asarray(v, dtype=f))
    x_prompt, x_sample, state_ssm, state_conv = a(x_prompt), a(x_sample), a(state_ssm), a(state_conv)
    c_prompt, c_sample = a(c_prompt), a(c_sample)
    if "nc" not in _CACHE:
        _CACHE["nc"] = build_program(NT)
    nc = _CACHE["nc"]

    def col(v, n):
        return a(v).reshape(n, 128).T

    wmod_t = _tile_w(a(w_mod)[0], [i * 512 for i in range(24)], 512)
    win = a(w_in)[0]
    wina_t = _tile_w(win, [1024, 1536, 0, 512, 3600, 4112, 2576, 3088], 512)
    winb_t = _tile_w(win, [2048], 528)
    wout_t = _tile_w(a(w_out)[0], [i * 512 for i in range(4)], 512)
    wff1_t = _tile_w(a(w_ff1)[0], [i * 512 for i in range(16)], 512)
    wff2_t = _tile_w(a(w_ff2)[0], [i * 512 for i in range(4)], 512)

    pcols = np.zeros((128, PC_N), f)
    pcols[:, PC_GIN:PC_GIN + 16] = col(ln_in_g, 16)
    pcols[:, PC_BIN:PC_BIN + 16] = col(ln_in_b, 16)
    pcols[:, PC_BMOD:PC_BMOD + 96] = col(a(b_mod)[0], 96)
    cw = a(conv_w)[0]
    pcols[:, PC_CW:PC_CW + 48] = cw.reshape(4, 12, 128).transpose(2, 1, 0).reshape(128, 48)
    pcols[:, PC_CB:PC_CB + 12] = col(a(conv_b)[0], 12)
    pcols[:, PC_NG:PC_NG + 8] = col(a(ssd_norm_g)[0], 8)
    pcols[:, PC_GMIX:PC_GMIX + 16] = col(a(ln_mix_g)[0], 16)
    pcols[:, PC_BMIX:PC_BMIX + 16] = col(a(ln_mix_b)[0], 16)
    pcols[:, PC_GFFN:PC_GFFN + 16] = col(a(ln_ffn_g)[0], 16)
    pcols[:, PC_BFFN:PC_BFFN + 16] = col(a(ln_ffn_b)[0], 16)
    prow = np.zeros((128, PR_N), f)
    prow[:, PR_DTB:PR_DTB + 16] = a(dt_bias)[0][None]
    prow[:, PR_ALOG:PR_ALOG + 16] = a(a_log)[0][None]
    prow[:, PR_DSK:PR_DSK + 16] = a(d_skip)[0][None]
    prow[:, PR_GG:PR_GG + 1024] = a(gm_ln_g)[0][None]
    prow[:, PR_GB:PR_GB + 1024] = a(gm_ln_b)[0][None]
    bs = a(gm_b_s)[0]
    idx = np.arange(128)
    bsT = np.concatenate([bs.T, bs[:, idx % 8].T], axis=1)
    ws = a(gm_w_s)[0]
    wsT = np.empty((128, 16, 128), f)
    wsT[:, 0:8, :] = ws.transpose(2, 0, 1)
    wsT[:, 8:16, :] = ws[:, idx % 8][:, :, idx % 8].transpose(2, 0, 1)
    cst = np.zeros((128, CS_N), f)
    ii = idx[:, None]
    jj = idx[None, :]
    cst[:, CS_ID:CS_ID + 128] = np.eye(128)
    cst[:, CS_U:CS_U + 128] = (ii > jj)
    cst[:, CS_L:CS_L + 128] = (ii <= jj)
    cst[:, CS_ONE:CS_ONE + 128] = 1.0
    cst[:, CS_LB:CS_LB + 128] = (ii <= jj) & (ii // 8 == jj // 8)
    cst[:, CS_BO:CS_BO + 128] = (ii // 8 == jj // 8)
    cst[:, CS_BSEL:CS_BSEL + 16] = (ii // 8 == np.arange(16)[None, :])
    colsel = np.broadcast_to((np.arange(16)[:, None] == (idx // 8)[None, :])[None], (128, 16, 128)).astype(f)

    in_maps = []
    for c in range(8):
        b, half = c // 2, c % 2
        xo = x_prompt[b, half * 1024:(half + 1) * 1024]
        xs = x_sample[16 * c:16 * c + 16].reshape(128, D)
        xT = np.ascontiguousarray(np.concatenate([xo, xs], 0).T)
        xpT = np.ascontiguousarray(x_prompt[b, 0:1024].T)
        cT = np.concatenate([c_prompt[b][None], c_sample[16 * c:16 * c + 16]], 0)
        cT = np.ascontiguousarray(cT.reshape(17, 16, 128).transpose(2, 1, 0))
        stT = np.ascontiguousarray(state_ssm[0, 16 * c:16 * c + 16].reshape(16, 1024, 128).transpose(0, 2, 1))
        cvT = np.ascontiguousarray(state_conv[0, 16 * c:16 * c + 16].reshape(16, 3, 12, 128).transpose(3, 2, 0, 1))
        in_maps.append({
            "xT": xT, "xpT": xpT, "gate": np.full((128, 1), float(half), f), "cT": cT,
            "wmod": wmod_t, "wina": wina_t, "winb": winb_t, "wout": wout_t, "wff1": wff1_t, "wff2": wff2_t,
            "pcols": pcols, "prow": prow, "bsT": np.ascontiguousarray(bsT), "wsT": wsT, "cst": cst, "colsel": colsel,
            "stT": stT, "cvT": cvT,
        })
    res = run_bass_kernel_spmd(nc, in_maps, core_ids=list(range(8)))
    R = res.results
    y_prompt = np.empty((4, 2048, D), f)
    y_sample = np.empty((128, 8, D), f)
    ssm_p = np.empty((1, 4, 16, 64, 128), f)
    conv_p = np.empty((1, 4, 3, 1536), f)
    ssm_s = np.empty((1, 128, 16, 64, 128), f)
    conv_s = np.empty((1, 128, 3, 1536), f)
    v_s = np.empty((1, 128, 8, 1024), f)
    for c in range(8):
        b, half = c // 2, c % 2
        yT = np.asarray(R[c]["yT"])
        y_prompt[b, half * 1024:(half + 1) * 1024] = yT[:, 0:1024].T
        y_sample[16 * c:16 * c + 16] = yT[:, 1024:1152].T.reshape(16, 8, D)
        if half == 1:
            ssm_p[0, b] = np.asarray(R[c]["ssmT"]).T.reshape(16, 64, 128)
            conv_p[0, b] = np.asarray(R[c]["convT"]).transpose(2, 1, 0).reshape(3, 1536)
        ssm_s[0, 16 * c:16 * c + 16] = np.asarray(R[c]["ssmsT"]).transpose(0, 2, 1).reshape(16, 16, 64, 128)
        conv_s[0, 16 * c:16 * c + 16] = np.asarray(R[c]["convsT"]).transpose(2, 3, 1, 0).reshape(16, 3, 1536)
        v_s[0, 16 * c:16 * c + 16] = np.asarray(R[c]["vs"]).reshape(16, 8, 1024)
    return (y_prompt, y_sample, ssm_p, conv_p, ssm_s, conv_s, v_s)
```

```python
import math
from contextlib import ExitStack
import numpy as np
import concourse.bass as bass
import concourse.mybir as mybir
from concourse.bass_utils import run_bass_kernel_spmd

F32 = mybir.dt.float32
BF16 = mybir.dt.bfloat16
AF = mybir.ActivationFunctionType
ALU = mybir.AluOpType

D = 2048
NCH = 16
DFF = 8192
DIN = 4624
ALPHA = 2.0 ** 0.25
EPS = 1e-5
NT = 3
RING = 6
CGW = 528

PC_GIN, PC_BIN, PC_BMOD, PC_CW, PC_CB, PC_NG, PC_GMIX, PC_BMIX, PC_GFFN, PC_BFFN = 0, 16, 32, 128, 176, 188, 196, 212, 228, 244
PC_N = 260
PR_DTB, PR_ALOG, PR_DSK, PR_GG, PR_GB = 0, 16, 32, 48, 48 + 1024
PR_N = 48 + 2048
CS_ID, CS_U, CS_L, CS_ONE, CS_LB, CS_BO, CS_BSEL = 0, 128, 256, 384, 512, 640, 768
CS_N = 784


class Op:
    __slots__ = ("eng", "fn", "deps", "is_dma", "sig", "sigval", "sem")


class Sched:
    ENGS = ["pe", "act", "dve", "pool", "sp"]

    def __init__(self):
        self.ops = {e: [] for e in self.ENGS}
        self.lastw = {}
        self.reads = {}
        self.dma_count = {e: 0 for e in self.ENGS}
        self.dma_prev = {}
        self.NDMA = 8
        self.pending = {}
        self.rd_dmas = []

    def add(self, eng, fn, reads=(), writes=(), dma=False):
        op = Op()
        op.eng, op.fn, op.is_dma, op.sig, op.sigval, op.sem = eng, fn, dma, False, 0, None
        deps = []
        for k in reads:
            w = self.lastw.get(k)
            if w is not None:
                deps.append(w)
            if isinstance(k, tuple) and k[0] in ("pf", "pb"):
                deps.extend(r for r in self.reads.get(k, ()) if r.eng != eng)
        for k in writes:
            w = self.lastw.get(k)
            if w is not None:
                deps.append(w)
            deps.extend(self.reads.get(k, ()))
        if self.pending.get(eng) is not None and (not dma or eng == "sp"):
            deps.append(self.pending[eng])
            self.pending[eng] = None
        if dma and len(reads) > 0:
            self.rd_dmas.append(op)
        if dma:
            op.sem = (eng, self.dma_count[eng] % self.NDMA)
            op.sigval = 16 * (self.dma_count[eng] // self.NDMA + 1)
            prev = self.dma_prev.get(op.sem)
            if prev is not None:
                deps.append(prev)
            self.dma_prev[op.sem] = op
            self.dma_count[eng] += 1
        seen = set()
        op.deps = []
        for d in deps:
            if d is op or id(d) in seen:
                continue
            seen.add(id(d))
            op.deps.append(d)
        for k in writes:
            self.lastw[k] = op
            self.reads[k] = []
        for k in reads:
            if k in writes:
                continue
            lst = self.reads.setdefault(k, [])
            if not dma:
                lst[:] = [r for r in lst if r.is_dma or r.eng != eng]
            lst.append(op)
        self.ops[eng].append(op)
        return op

    def barrier(self, fn):
        deps = []
        for e in self.ENGS:
            for o in reversed(self.ops[e]):
                if not o.is_dma:
                    deps.append(o)
                    break
        deps.extend(self.rd_dmas)
        self.rd_dmas = []
        op = self.add("dve", fn)
        for d in deps:
            if d is not op and d not in op.deps:
                op.deps.append(d)
        for e in ["pe", "act", "pool", "sp"]:
            self.pending[e] = op
        return op

    def _skip(self, d, e):
        return (not d.is_dma) and d.eng == e and e == "pe"

    def finalize(self):
        for e in self.ENGS:
            for op in self.ops[e]:
                for d in op.deps:
                    if d.is_dma or self._skip(d, e):
                        continue
                    d.sig = True
        for e in self.ENGS:
            c = 0
            for op in self.ops[e]:
                if (not op.is_dma) and op.sig:
                    c += 1
                    op.sigval = c

    def emit(self, block, sems, dsems):
        engmap = {"pe": block.tensor, "act": block.scalar, "dve": block.vector,
                  "pool": block.gpsimd, "sp": block.sync}
        for e in self.ENGS:
            ops = self.ops[e]
            if not ops:
                continue

            def body(eng, e=e, ops=ops):
                waited = {}
                for op in ops:
                    for d in op.deps:
                        if d.is_dma:
                            key, sem, val = ("d",) + d.sem, dsems[d.sem], d.sigval
                        else:
                            if self._skip(d, e):
                                continue
                            key, sem, val = ("c", d.eng), sems[d.eng], d.sigval
                        if waited.get(key, 0) >= val:
                            continue
                        waited[key] = val
                        eng.wait_ge(sem, val)
                    ins = op.fn(eng)
                    if op.is_dma:
                        ins.then_inc(dsems[op.sem], 16)
                    elif op.sig:
                        ins.then_inc(sems[e], 1)

            engmap[e](body)


def build_program(nt=NT):
    nc = bass.Bass("TRN2", target_bir_lowering=False)

    def din(name, shape):
        return nc.dram_tensor(name, list(shape), F32, kind="ExternalInput").ap()

    def dout(name, shape):
        return nc.dram_tensor(name, list(shape), F32, kind="ExternalOutput").ap()

    xT_d = din("xT", [D, 1152])
    xpT_d = din("xpT", [D, 1024])
    gate_d = din("gate", [128, 1])
    cT_d = din("cT", [128, 16, 17])
    wmod_d = din("wmod", [24, 8, 128, 2, 512])
    wina_d = din("wina", [8, 8, 128, 2, 512])
    winb_d = din("winb", [1, 8, 128, 2, 528])
    wout_d = din("wout", [4, 8, 128, 2, 512])
    wff1_d = din("wff1", [16, 8, 128, 2, 512])
    wff2_d = din("wff2", [4, 32, 128, 2, 512])
    pcols_d = din("pcols", [128, PC_N])
    prow_d = din("prow", [128, PR_N])
    bsT_d = din("bsT", [128, 16])
    wsT_d = din("wsT", [128, 16, 128])
    cst_d = din("cst", [128, CS_N])
    colsel_d = din("colsel", [128, 16, 128])
    stT_d = din("stT", [16, 128, 1024])
    cvT_d = din("cvT", [128, 12, 16, 3])

    yT_d = dout("yT", [D, 1152])
    ssmT_d = dout("ssmT", [128, 1024])
    convT_d = dout("convT", [128, 12, 3])
    ssmsT_d = dout("ssmsT", [16, 128, 1024])
    convsT_d = dout("convsT", [128, 12, 16, 3])
    vs_d = dout("vs", [128, 1024])

    S = Sched()
    TBM = 128 * nt

    with ExitStack() as es:
        def sb(name, shape, dt=F32):
            return es.enter_context(nc.sbuf_tensor("s_" + name, list(shape), dt))

        def ps(name, shape, dt=F32):
            return es.enter_context(nc.psum_tensor(name, list(shape), dt))

        sems = {e: es.enter_context(nc.semaphore("s_" + e)) for e in Sched.ENGS}
        dsems = {(e, i): es.enter_context(nc.semaphore(f"d_{e}_{i}"))
                 for e in ["sp", "pool"] for i in range(S.NDMA)}

        cst = sb("cst", [128, CS_N])
        ident = cst[:, CS_ID:CS_ID + 128]
        Um = cst[:, CS_U:CS_U + 128]
        Lm = cst[:, CS_L:CS_L + 128]
        ones = cst[:, CS_ONE:CS_ONE + 128]
        LmB = cst[:, CS_LB:CS_LB + 128]
        BlkO = cst[:, CS_BO:CS_BO + 128]
        Bsel = cst[:, CS_BSEL:CS_BSEL + 16]
        identb = sb("identb", [128, 128], BF16)
        colsel = sb("colsel", [128, 16, 128], BF16)
        pcols = sb("pcols", [128, PC_N])
        prow = sb("prow", [128, PR_N])
        bsT = sb("bsT", [128, 16])
        WT = sb("WT", [128, 16, 128], BF16)
        gate = sb("gate", [128, 1])
        scT = sb("scT", [128, 16, 17], BF16)
        sclh = sb("sclh", [128, 16, 17])
        bish = sb("bish", [128, 16, 17])
        sclh2 = sb("sclh2", [128, 16, 17])
        bish2 = sb("bish2", [128, 16, 17])
        Gm = sb("Gm", [128, 16, 17])
        Gf = sb("Gf", [128, 16, 17])
        agin = sb("agin", [128, 64])
        Arow = sb("Arow", [128, 16])
        cvT = sb("cvT", [128, 12, 16, 3])
        convs = sb("convs", [128, 12, 16, 3])
        tail = sb("tail", [128, 12, 3])
        ST = sb("ST", [128, 1024])
        STb = sb("STb", [128, 1024], BF16)

        xA = sb("xA", [128, 16, TBM])
        hT = sb("hT", [128, 16, TBM], BF16)
        assert nt == 3
        AR = 4352 * nt
        arena = sb("arena", [128, AR])
        o_yy, o_gv, o_xc, o_xtm, o_mix = 0, 1024 * nt, 2048 * nt, 2048 * nt + 768 * nt, 2048 * nt + 1280 * nt
        yy = arena[:, o_yy:o_yy + 1024 * nt].rearrange("p (t f) -> p t f", f=1024)
        gv = arena[:, o_gv:o_gv + 1024 * nt].rearrange("p (t f) -> p t f", f=1024)
        xc = arena[:, o_xc:o_xc + 768 * nt].bitcast(BF16).rearrange("p (c t) -> p c t", t=TBM)
        x_tm = arena[:, o_xtm:o_xtm + 512 * nt].bitcast(BF16).rearrange("p (t f) -> p t f", f=1024)
        mixT = arena[:, o_mix:o_mix + 1024 * nt].bitcast(BF16).rearrange("p (c t) -> p c t", t=TBM)
        hid = arena[:, 0:4096 * nt].bitcast(BF16).rearrange("p (c t) -> p c t", t=TBM)
        stin = [arena[:, o_mix + i * 1024:o_mix + (i + 1) * 1024] for i in range(2)]
        snew = arena[:, o_mix + 2048:o_mix + 3072]
        modT = arena[:, 0:1632].rearrange("p (c k) -> p c k", k=17)
        wstmp = arena[:, 1632:1632 + 2048].rearrange("p (c k) -> p c k", k=128)
        ctmp = arena[:, 3680:3680 + 272].rearrange("p (c k) -> p c k", k=17)
        ctmp2 = arena[:, 3952:3952 + 272].rearrange("p (c k) -> p c k", k=17)
        dummy = sb("dummy", [128, 8])
        ring = [sb(f"ring{i}", [128, 2, CGW], BF16) for i in range(RING)]
        sq = [sb(f"sq{i}", [128, TBM], BF16) for i in range(2)]
        onesb = sb("onesb", [128, 128], BF16)
        mean = sb("mean", [128, TBM])
        rstd = sb("rstd", [128, TBM])
        vtmp = sb("vtmp", [128, TBM])
        xn = [sb(f"xn{i}", [128, TBM]) for i in range(2)]
        rl = [sb(f"rl{i}", [128, TBM]) for i in range(2)]
        stg2 = [sb(f"stg{i}", [128, 4, 3 + TBM]) for i in range(2)]
        sstg2 = [sb(f"sstg{i}", [128, 4, 16, 11]) for i in range(2)]
        cacc2 = [sb(f"cacc{i}", [128, TBM]) for i in range(2)]
        B_tm = sb("B_tm", [128, nt, 256], BF16)
        smt = [sb(f"sm{i}", [128, 12, 16]) for i in range(nt)]
        lh = [sb(f"lh{i}", [128, 4, 128]) for i in range(2)]
        Ee = [sb(f"Ee{i}", [128, 4, 128]) for i in range(2)]
        Mm = sb("Mm", [128, 16, 128], BF16)
        CBm = sb("CBm", [128, 2, 128], BF16)
        xdt = sb("xdt", [128, 1024], BF16)
        xdd = sb("xdd", [128, 1024], BF16)
        xDs = sb("xDs", [128, 1024], BF16)
        vnb = sb("vnb", [128, 1024], BF16)
        gut = sb("gut", [128, 512])
        ysb = sb("ysb", [128, 1024], BF16)
        bstt = sb("bstt", [128, nt, 12])
        bmvt = sb("bmvt", [128, nt, 2])
        ss2t = sb("ss2t", [128, nt, 4])
        ablk = sb("ablk", [128, 16, 16])
        decall = sb("decall", [128, 16, 16])
        stbf2 = [sb(f"stbf{i}", [128, 1024], BF16) for i in range(2)]
        Cmk2 = [sb(f"Cmk{i}", [128, 2, 128], BF16) for i in range(2)]
        Bmk2 = [sb(f"Bmk{i}", [128, 256], BF16) for i in range(2)]

        pf = [ps(f"pf{i}", [128, 512]) for i in range(6)]
        pbk = [ps(f"pb{i}", [128, 1024], BF16) for i in range(2)]
        bank_ctr = [0]
        pool_n = [4]
        pb_ctr = [0]

        def bank():
            i = bank_ctr[0] % pool_n[0]
            bank_ctr[0] += 1
            return pf[i], ("pf", i)

        def bbank():
            i = pb_ctr[0] % 2
            pb_ctr[0] += 1
            return pbk[i], ("pb", i)

        ring_ctr = [0]

        def V(fn, r, w):
            return S.add("dve", fn, r, w)

        def A(fn, r, w):
            return S.add("act", fn, r, w)

        def G(fn, r, w):
            return S.add("pool", fn, r, w)

        def PE(fn, r, w):
            return S.add("pe", fn, r, w)

        def DM(q, fn, r, w):
            return S.add(q, fn, r, w, dma=True)

        DM("sp", lambda e: e.dma_start(out=cst[:], in_=cst_d), [], ["cst"])
        DM("sp", lambda e: e.dma_start(out=pcols[:], in_=pcols_d), [], ["pcols"])
        DM("sp", lambda e: e.dma_start(out=prow[:], in_=prow_d), [], ["prow"])
        DM("sp", lambda e: e.dma_start(out=bsT[:], in_=bsT_d), [], ["bsT"])
        DM("sp", lambda e: e.dma_start(out=gate[:], in_=gate_d), [], ["gate"])
        DM("sp", lambda e: e.dma_start(out=ctmp, in_=cT_d), [], ["ctmp"])
        DM("sp", lambda e: e.dma_start(out=wstmp, in_=wsT_d), [], ["wstmp"])
        DM("sp", lambda e: e.dma_start(out=cvT[:], in_=cvT_d), [], ["cvT"])
        DM("pool", lambda e: e.dma_start(out=colsel[:], in_=colsel_d), [], ["colsel"])
        V(lambda e: e.tensor_copy(out=identb[:], in_=ident), ["cst"], ["identb"])
        V(lambda e: e.tensor_copy(out=onesb[:], in_=ones), ["cst"], ["onesb"])
        V(lambda e: e.tensor_tensor(out=WT[:, 0:8, :], in0=wstmp[:, 0:8, :],
                                    in1=Lm.unsqueeze(1).to_broadcast([128, 8, 128]), op=ALU.mult),
          ["cst", "wstmp"], ["WT"])
        V(lambda e: e.tensor_tensor(out=WT[:, 8:16, :], in0=wstmp[:, 8:16, :],
                                    in1=LmB.unsqueeze(1).to_broadcast([128, 8, 128]), op=ALU.mult),
          ["cst", "wstmp"], ["WT"])
        A(lambda e: e.activation(out=scT[:], in_=ctmp, func=AF.Silu), ["ctmp"], ["scT"])
        V(lambda e: e.memset(ST[:], 0.0), [], ["ST"])
        V(lambda e: e.memset(STb[:], 0.0), [], ["STb"])
        V(lambda e: e.memset(tail[:], 0.0), [], ["tail"])
        A(lambda e: e.activation(out=Arow[:], in_=prow[:, PR_ALOG:PR_ALOG + 16], func=AF.Exp), ["prow"], ["Arow"])
        V(lambda e: e.tensor_scalar(out=Arow[:], in0=Arow[:], scalar1=-1.0, scalar2=None, op0=ALU.mult),
          ["Arow"], ["Arow"])
        V(lambda e: e.tensor_scalar(out=agin[:, 0:32], in0=pcols[:, PC_GIN:PC_GIN + 32], scalar1=ALPHA, scalar2=None,
                                    op0=ALU.mult), ["pcols"], ["agin"])
        V(lambda e: e.tensor_scalar(out=agin[:, 32:64], in0=pcols[:, PC_GMIX:PC_GMIX + 32], scalar1=ALPHA, scalar2=None,
                                    op0=ALU.mult), ["pcols"], ["agin"])

        def stream(wd, ncg, nkp, cgw, consumer, cgs=None):
            for cg in (range(ncg) if cgs is None else cgs):
                for kp in range(nkp):
                    slot = ring_ctr[0] % RING
                    ring_ctr[0] += 1
                    rk = ("ring", slot)
                    DM("pool", lambda e, slot=slot, cg=cg, kp=kp: e.dma_start(out=ring[slot][:, :, 0:cgw], in_=wd[cg, kp]),
                       [], [rk])
                    for kk in range(2):
                        consumer(cg, kp * 2 + kk, ring[slot][:, kk, 0:cgw], rk)

        modbanks = {}

        def mod_cons(cg, k, w, rk):
            if k == 0:
                modbanks[cg] = [bank() for _ in range(4)]
            for j in range(4):
                bk, bkey = modbanks[cg][j]
                PE(lambda e, bk=bk, j=j, k=k, w=w: e.matmul(bk[:, 0:17], lhsT=w[:, j * 128:(j + 1) * 128], rhs=scT[:, k, :],
                                                        start=(k == 0), stop=(k == 15)), [rk, "scT"], [bkey])
            if k == 15:
                for j in range(4):
                    bk, bkey = modbanks[cg][j]
                    ch = cg * 4 + j
                    V(lambda e, bk=bk, ch=ch: e.tensor_scalar(out=modT[:, ch, :], in0=bk[:, 0:17],
                                                             scalar1=pcols[:, PC_BMOD + ch:PC_BMOD + ch + 1], scalar2=None,
                                                             op0=ALU.add), [bkey, "pcols"], [("modT", ch // 16)])

        mod_left = list(range(24))

        def emit_mod(n):
            for _ in range(n):
                if not mod_left:
                    return
                stream(wmod_d, 24, 8, 512, mod_cons, cgs=[mod_left.pop(0)])

        def bc17(col0):
            return pcols[:, col0:col0 + 16].unsqueeze(2).to_broadcast([128, 16, 17])

        def setup_mod1():
            emit_mod(8)
            V(lambda e: e.tensor_scalar(out=ctmp, in0=modT[:, 16:32, :], scalar1=1.0, scalar2=None, op0=ALU.add),
              [("modT", 1)], ["ctmp"])
            V(lambda e: e.tensor_tensor(out=sclh[:], in0=ctmp, in1=bc17(PC_GIN), op=ALU.mult), ["ctmp", "pcols"], ["sclh"])
            V(lambda e: e.tensor_tensor(out=ctmp2, in0=ctmp, in1=bc17(PC_BIN), op=ALU.mult), ["ctmp", "pcols"], ["ctmp2"])
            V(lambda e: e.tensor_tensor(out=bish[:], in0=ctmp2, in1=modT[:, 0:16, :], op=ALU.add),
              ["ctmp2", ("modT", 0)], ["bish"])

        def ln_stats(src, TB, skey):
            b1, k1 = bank()
            b2, k2 = bank()
            for c in range(16):
                sqt = sq[c % 2]
                sqk = ("sq", c % 2)
                A(lambda e, c=c, sqt=sqt: e.activation(out=sqt[:, 0:TB], in_=src[:, c, 0:TB], func=AF.Square),
                  [(skey, c)], [sqk])
                PE(lambda e, c=c: e.matmul(b1[:, 0:TB], lhsT=ones, rhs=src[:, c, 0:TB], start=(c == 0), stop=(c == 15)),
                   [(skey, c), "cst"], [k1])
                PE(lambda e, c=c, sqt=sqt: e.matmul(b2[:, 0:TB], lhsT=onesb[:], rhs=sqt[:, 0:TB], start=(c == 0), stop=(c == 15)),
                   [sqk, "onesb"], [k2])
            V(lambda e: e.tensor_scalar(out=mean[:, 0:TB], in0=b1[:, 0:TB], scalar1=1.0 / D, scalar2=None, op0=ALU.mult),
              [k1], ["mean"])
            V(lambda e: e.tensor_tensor(out=vtmp[:, 0:TB], in0=mean[:, 0:TB], in1=mean[:, 0:TB], op=ALU.mult),
              ["mean"], ["vtmp"])
            V(lambda e: e.scalar_tensor_tensor(out=vtmp[:, 0:TB], in0=b2[:, 0:TB], scalar=1.0 / D, in1=vtmp[:, 0:TB],
                                               op0=ALU.mult, op1=ALU.subtract), [k2, "vtmp"], ["vtmp"])
            V(lambda e: e.tensor_scalar(out=vtmp[:, 0:TB], in0=vtmp[:, 0:TB], scalar1=EPS, scalar2=None, op0=ALU.add),
              ["vtmp"], ["vtmp"])
            A(lambda e: e.activation(out=vtmp[:, 0:TB], in_=vtmp[:, 0:TB], func=AF.Sqrt), ["vtmp"], ["vtmp"])
            V(lambda e: e.reciprocal(out=rstd[:, 0:TB], in_=vtmp[:, 0:TB]), ["vtmp"], ["rstd"])

        def ln_apply(src, TB, skey, fn):
            for cb_ in range(0, 16, 2):
                for i in range(2):
                    V(lambda e, i=i, c=cb_ + i: e.tensor_tensor(out=xn[i][:, 0:TB], in0=src[:, c, 0:TB], in1=mean[:, 0:TB],
                                                              op=ALU.subtract), [(skey, cb_ + i), "mean"], [("xn", i)])
                for i in range(2):
                    V(lambda e, i=i: e.tensor_tensor(out=xn[i][:, 0:TB], in0=xn[i][:, 0:TB], in1=rstd[:, 0:TB], op=ALU.mult),
                      [("xn", i), "rstd"], [("xn", i)])
                for i in range(2):
                    fn(cb_ + i, xn[i], ("xn", i))

        def mod_h(t, tk, c, TBp, TB, scl, bis, sk, bk_):
            if TBp > 0:
                A(lambda e: e.activation(out=hT[:, c, 0:TBp], in_=t[:, 0:TBp], func=AF.Identity,
                                         bias=bis[:, c, 0:1], scale=scl[:, c, 0:1]), [tk, sk, bk_], [("hT", c)])
            if TB > TBp:
                tv = t[:, TBp:TB].rearrange("p (b t) -> p b t", t=8)
                hv = hT[:, c, TBp:TB].rearrange("p (b t) -> p b t", t=8)
                V(lambda e: e.tensor_tensor(out=tv, in0=tv, in1=scl[:, c, 1:17].unsqueeze(2).to_broadcast([128, 16, 8]),
                                            op=ALU.mult), [tk, sk], [tk])
                V(lambda e: e.tensor_tensor(out=hv, in0=tv, in1=bis[:, c, 1:17].unsqueeze(2).to_broadcast([128, 16, 8]),
                                            op=ALU.add), [tk, bk_], [("hT", c)])

        def feat_linear(wd, ncg, K, act, akey, TB, evac):
            banks = {}

            def cons(cg, k, w, rk):
                if k == 0:
                    banks[cg] = [bank() for _ in range(4)]
                for j in range(4):
                    bk, bkey = banks[cg][j]
                    PE(lambda e, bk=bk, j=j, k=k, w=w: e.matmul(bk[:, 0:TB], lhsT=w[:, j * 128:(j + 1) * 128],
                                                            rhs=act[:, k, 0:TB], start=(k == 0), stop=(k == K - 1)),
                       [rk, (akey, k)], [bkey])
                if k == K - 1:
                    for j in range(4):
                        bk, bkey = banks[cg][j]
                        evac(cg * 4 + j, bk, bkey)

            stream(wd, ncg, K // 2, 512, cons)

        def run_block(src_d, col0, ntp, has_s, mode, hook=lambda: None, pre_hook=None):
            TBp = 128 * ntp
            TB = TBp + (128 if has_s else 0)
            pool_n[0] = 4
            ntl = ntp + (1 if has_s else 0)
            full = mode == "full"
            for c in range(16):
                DM("sp", lambda e, c=c: e.dma_start(out=xA[:, c, 0:TB], in_=src_d[c * 128:(c + 1) * 128, col0:col0 + TB]),
                   [], [("xA", c)])
            ln_stats(xA, TB, "xA")
            def fnA(c, t, tk):
                if full:
                    A(lambda e, c=c, t=t: e.activation(out=xA[:, c, 0:TB], in_=t[:, 0:TB], func=AF.Identity,
                                                       scale=agin[:, c:c + 1], bias=agin[:, 16 + c:17 + c]),
                      [tk, "agin"], [("xA", c)])
                mod_h(t, tk, c, TBp, TB, sclh, bish, "sclh", "bish")

            if full:
                ln_apply(xA, TB, "xA", fnA)
            else:
                for cb_ in range(0, 16, 2):
                    for c in (cb_, cb_ + 1):
                        V(lambda e, c=c: e.tensor_tensor(out=xA[:, c, 0:TB], in0=xA[:, c, 0:TB], in1=mean[:, 0:TB], op=ALU.subtract),
                          [("xA", c), "mean"], [("xA", c)])
                    for c in (cb_, cb_ + 1):
                        V(lambda e, c=c: e.tensor_tensor(out=xA[:, c, 0:TB], in0=xA[:, c, 0:TB], in1=rstd[:, 0:TB], op=ALU.mult),
                          [("xA", c), "rstd"], [("xA", c)])
                if pre_hook is not None:
                    pre_hook()
                for c in range(16):
                    A(lambda e, c=c: e.activation(out=hT[:, c, 0:TB], in_=xA[:, c, 0:TB], func=AF.Identity,
                                                  bias=bish[:, c, 0:1], scale=sclh[:, c, 0:1]),
                      [("xA", c), "sclh", "bish"], [("hT", c)])
            hook()

            dtb, dtk = pf[4], ("pf", 4)

            def conv_chunk(ch, j, bk, bkey):
                par = (ch // 4) % 2
                stg = stg2[par]
                sstg = sstg2[par]
                j2 = j
                j = (par, j2)
                return conv_chunk_(ch, j2, j, stg, sstg, bk, bkey)

            def run_rr(gl):
                act_ = list(gl)
                while act_:
                    for g_ in list(act_):
                        try:
                            next(g_)
                        except StopIteration:
                            act_.remove(g_)

            def conv_chunk_(ch, jj, j, stg, sstg, bk, bkey):
                cacc = cacc2[jj % 2]
                ck = ("cacc", jj % 2)
                cw = pcols[:, PC_CW + ch * 4:PC_CW + ch * 4 + 4]
                cb = pcols[:, PC_CB + ch:PC_CB + ch + 1]
                if TBp > 0:
                    A(lambda e: e.activation(out=cacc[:, 0:TBp], in_=bk[:, 0:TBp], func=AF.Identity, scale=cw[:, 3:4], bias=cb),
                      [bkey, "pcols"], [ck])
                    A(lambda e: e.activation(out=stg[:, jj, 3:3 + TBp], in_=bk[:, 0:TBp], func=AF.Identity),
                      [bkey], [("stg", j)])
                    yield
                    A(lambda e: e.activation(out=stg[:, jj, 0:3], in_=tail[:, ch, :], func=AF.Identity), [("tail", ch)], [("stg", j)])
                    yield
                    for k in range(0, 3):
                        V(lambda e, k=k: e.scalar_tensor_tensor(out=cacc[:, 0:TBp], in0=stg[:, jj, k:k + TBp],
                                                                scalar=cw[:, k:k + 1], in1=cacc[:, 0:TBp],
                                                                op0=ALU.mult, op1=ALU.add), [("stg", j), ck], [ck])
                        yield
                    A(lambda e: e.activation(out=xc[:, ch, 0:TBp], in_=cacc[:, 0:TBp], func=AF.Silu), [ck], [("xc", ch)])
                    yield
                    A(lambda e: e.activation(out=tail[:, ch, :], in_=stg[:, jj, TBp:TBp + 3], func=AF.Identity), [("stg", j)], [("tail", ch)])
                if has_s:
                    A(lambda e: e.activation(out=sstg[:, jj, :, 3:11], in_=bk[:, TBp:TB].rearrange("p (b t) -> p b t", t=8),
                                             func=AF.Identity), [bkey], [("sstg", j)])
                    A(lambda e: e.activation(out=sstg[:, jj, :, 0:3], in_=cvT[:, ch, :, :], func=AF.Identity), ["cvT"], [("sstg", j)])
                    cv = cacc[:, 0:128].rearrange("p (b t) -> p b t", t=8)
                    A(lambda e: e.activation(out=cv, in_=bk[:, TBp:TB].rearrange("p (b t) -> p b t", t=8), func=AF.Identity,
                                             scale=cw[:, 3:4], bias=cb), [bkey, "pcols", ("xc", ch)], [ck])
                    yield
                    for k in range(0, 3):
                        V(lambda e, k=k: e.scalar_tensor_tensor(out=cv, in0=sstg[:, jj, :, k:k + 8], scalar=cw[:, k:k + 1],
                                                                in1=cv, op0=ALU.mult, op1=ALU.add),
                          [("sstg", j), ck], [ck])
                        yield
                    A(lambda e: e.activation(out=xc[:, ch, TBp:TB].rearrange("p (b t) -> p b t", t=8), in_=cv, func=AF.Silu),
                      [ck], [("xc", ch)])
                    A(lambda e: e.activation(out=convs[:, ch, :, :], in_=sstg[:, jj, :, 8:11], func=AF.Identity), [("sstg", j)], ["convs"])

            xbanks = {}

            def xbc_cons(cg, k, w, rk, base):
                g = base + cg
                if k == 0:
                    xbanks[g] = [bank() for _ in range(4)]
                for j in range(4):
                    bk, bkey = xbanks[g][j]
                    PE(lambda e, bk=bk, j=j, k=k, w=w: e.matmul(bk[:, 0:TB], lhsT=w[:, j * 128:(j + 1) * 128],
                                                            rhs=hT[:, k, 0:TB], start=(k == 0), stop=(k == 15)),
                       [rk, ("hT", k)], [bkey])
                if g == 2:
                    for t in range(ntl):
                        PE(lambda e, t=t, k=k, w=w: e.matmul(dtb[:, t * 16:(t + 1) * 16], lhsT=hT[:, k, t * 128:(t + 1) * 128],
                                                         rhs=w[:, 512:528], start=(k == 0 and t == 0), stop=(k == 15),
                                                         skip_group_check=True),
                           [rk, ("hT", k)], [dtk])
                if k == 15:
                    for jp in (0, 2):
                        run_rr([conv_chunk(g * 4 + j, j, xbanks[g][j][0], xbanks[g][j][1]) for j in (jp, jp + 1)])
                    hook()

            stream(wina_d[0:2], 2, 8, 512, lambda cg, k, w, rk: xbc_cons(cg, k, w, rk, 0))
            stream(winb_d, 1, 8, 528, lambda cg, k, w, rk: xbc_cons(cg, k, w, rk, 2))

            def tok_linear(gbase, evac):
                tb = {}

                def cons(cg, k, w, rk):
                    if k == 0:
                        tb[cg] = [bank() for _ in range(ntl)]
                    for t in range(ntl):
                        bk, bkey = tb[cg][t]
                        PE(lambda e, bk=bk, t=t, k=k, w=w: e.matmul(bk[:, 0:512], lhsT=hT[:, k, t * 128:(t + 1) * 128], rhs=w,
                                                                start=(k == 0), stop=(k == 15)), [rk, ("hT", k)], [bkey])
                    if k == 15:
                        for t in range(ntl):
                            bk, bkey = tb[cg][t]
                            evac(cg, t, bk, bkey)

                stream(wina_d[gbase:gbase + 2], 2, 8, 512, cons)

            if full:
                tok_linear(4, lambda cg, t, bk, bkey: A(
                    lambda e: e.activation(out=gv[:, t, cg * 512:(cg + 1) * 512], in_=bk[:, 0:512], func=AF.Gelu_apprx_tanh),
                    [bkey], [("gv", t)]))

            def ssd_tile(t, is_s):
                c0 = t * 128
                sm = smt[t]

                def small(i):
                    return sm[:, i, :]
                Lmask = LmB if is_s else Lm
                Omask = BlkO if is_s else ones
                pb, pbkey = bbank()
                for ch in range(8):
                    PE(lambda e, ch=ch: e.transpose(out=pb[:, ch * 128:(ch + 1) * 128], in_=xc[:, ch, c0:c0 + 128], identity=identb[:]),
                       [("xc", ch), "identb"], [pbkey])
                A(lambda e: e.activation(out=x_tm[:, t, :], in_=pb[:, :], func=AF.Identity), [pbkey], [("x_tm", t)])
                yield "p1"
                pb2, pb2key = bbank()
                for g in range(2):
                    PE(lambda e, g=g: e.transpose(out=pb2[:, g * 128:(g + 1) * 128], in_=xc[:, 8 + g, c0:c0 + 128], identity=identb[:]),
                       [("xc", 8 + g), "identb"], [pb2key])
                A(lambda e: e.activation(out=B_tm[:, t, :], in_=pb2[:, 0:256], func=AF.Identity), [pb2key], [("B_tm", t)])
                yield "p1"
                dtr, mx, ab, ex, dtv, av, acs, dd, dend, cdec, eac, dtx = [small(i) for i in range(12)]
                V(lambda e: e.tensor_tensor(out=dtr, in0=dtb[:, t * 16:(t + 1) * 16], in1=prow[:, PR_DTB:PR_DTB + 16], op=ALU.add),
                  [dtk, "prow"], [("sm", t)])
                V(lambda e: e.tensor_scalar(out=mx, in0=dtr, scalar1=0.0, scalar2=None, op0=ALU.max), [("sm", t)], [("sm", t)])
                V(lambda e: e.tensor_scalar(out=ab, in0=dtr, scalar1=-1.0, scalar2=None, op0=ALU.mult), [("sm", t)], [("sm", t)])
                V(lambda e: e.tensor_tensor(out=ab, in0=ab, in1=dtr, op=ALU.max), [("sm", t)], [("sm", t)])
                yield "p1"
                A(lambda e: e.activation(out=ex, in_=ab, func=AF.Exp, scale=-1.0), [("sm", t)], [("sm", t)])
                yield "p1"
                A(lambda e: e.activation(out=ex, in_=ex, func=AF.Ln, bias=1.0), [("sm", t)], [("sm", t)])
                yield "p1"
                V(lambda e: e.tensor_tensor(out=dtv, in0=mx, in1=ex, op=ALU.add), [("sm", t)], [("sm", t)])
                V(lambda e: e.tensor_tensor(out=av, in0=dtv, in1=Arow[:], op=ALU.mult), [("sm", t), "Arow"], [("sm", t)])
                yield "p1"
                pa, pak = bank()
                PE(lambda e: e.matmul(pa[:, 0:16], lhsT=Lmask, rhs=av, start=True, stop=True), [("sm", t), "cst"], [pak])
                PE(lambda e: e.matmul(pa[:, 16:32], lhsT=Omask, rhs=av, start=True, stop=True), [("sm", t), "cst"], [pak])
                yield "p1"
                V(lambda e: e.tensor_copy(out=acs, in_=pa[:, 0:16]), [pak], [("sm", t)])
                V(lambda e: e.tensor_tensor(out=dd, in0=pa[:, 16:32], in1=acs, op=ALU.subtract), [pak, ("sm", t)], [("sm", t)])
                yield "p1"
                A(lambda e: e.activation(out=dend, in_=dd, func=AF.Exp), [("sm", t)], [("sm", t)])
                A(lambda e: e.activation(out=cdec, in_=pa[:, 16:32], func=AF.Exp), [pak], [("sm", t)])
                A(lambda e: e.activation(out=eac, in_=acs, func=AF.Exp), [("sm", t)], [("sm", t)])
                yield "p1"
                V(lambda e: e.tensor_tensor(out=dtx, in0=dtv, in1=dend, op=ALU.mult), [("sm", t)], [("sm", t)])

                yield "end1"

                def b64(v):
                    return v.unsqueeze(2).to_broadcast([128, 16, 64])

                x3 = x_tm[:, t, :].rearrange("p (h d) -> p h d", d=64)
                V(lambda e: e.tensor_tensor(out=xdd[:].rearrange("p (h d) -> p h d", d=64), in0=x3, in1=b64(dtx), op=ALU.mult),
                  [("x_tm", t), ("sm", t)], ["xdd"])
                if full:
                    V(lambda e: e.tensor_tensor(out=xdt[:].rearrange("p (h d) -> p h d", d=64), in0=x3, in1=b64(dtv), op=ALU.mult),
                      [("x_tm", t), ("sm", t)], ["xdt"])
                    V(lambda e: e.tensor_tensor(out=xDs[:].rearrange("p (h d) -> p h d", d=64), in0=x3,
                                                in1=b64(prow[:, PR_DSK:PR_DSK + 16]), op=ALU.mult),
                      [("x_tm", t), "prow"], ["xDs"])
                    pc, pck = bank()
                    for g in range(2):
                        PE(lambda e, g=g: e.matmul(pc[:, g * 128:(g + 1) * 128], lhsT=xc[:, 8 + g, c0:c0 + 128],
                                                   rhs=xc[:, 10 + g, c0:c0 + 128], start=True, stop=True),
                           [("xc", 8 + g), ("xc", 10 + g)], [pck])
                    V(lambda e: e.tensor_tensor(out=CBm[:], in0=pc[:, 0:256].rearrange("p (g i) -> p g i", i=128),
                                                in1=Lmask.unsqueeze(1).to_broadcast([128, 2, 128]), op=ALU.mult),
                      [pck, "cst"], ["CBm"])
                    for qa in (0, 2):
                        qs = (qa, qa + 1)
                        for q in qs:
                            V(lambda e, q=q: e.tensor_tensor(out=lh[q % 2][:], in0=Um.unsqueeze(1).to_broadcast([128, 4, 128]),
                                                             in1=av[:, 4 * q:4 * q + 4].unsqueeze(2).to_broadcast([128, 4, 128]),
                                                             op=ALU.mult), ["cst", ("sm", t)], [("lh", q % 2)])
                        pgs = {}
                        for q in qs:
                            pg, pgk = bank()
                            pgs[q] = (pg, pgk)
                            for hh in range(4):
                                PE(lambda e, hh=hh, q=q, pg=pg: e.matmul(pg[:, hh * 128:(hh + 1) * 128], lhsT=lh[q % 2][:, hh, :], rhs=Lm,
                                                                        start=True, stop=True), [("lh", q % 2), "cst"], [pgk])
                        for q in qs:
                            pg, pgk = pgs[q]
                            A(lambda e, q=q, pg=pg: e.activation(out=Ee[q % 2][:].rearrange("p a b -> p (a b)"), in_=pg[:, :], func=AF.Exp),
                              [pgk], [("Ee", q % 2)])
                        for q in qs:
                            V(lambda e, q=q: e.tensor_tensor(out=Mm[:, 4 * q:4 * q + 4, :], in0=Ee[q % 2][:],
                                                             in1=CBm[:, q // 2, :].unsqueeze(1).to_broadcast([128, 4, 128]),
                                                             op=ALU.mult), [("Ee", q % 2), "CBm"], [("Mm", q)])
                    if not is_s:
                        for g in range(2):
                            po, pok = bank()
                            PE(lambda e, g=g, po=po: e.matmul(po[:, :], lhsT=xc[:, 10 + g, c0:c0 + 128], rhs=STb[:, g * 512:(g + 1) * 512],
                                                              start=True, stop=True), [("xc", 10 + g), "STb"], [pok])
                            V(lambda e, g=g, po=po: e.tensor_tensor(
                                out=yy[:, t, g * 512:(g + 1) * 512].rearrange("p (h d) -> p h d", d=64),
                                in0=po[:, :].rearrange("p (h d) -> p h d", d=64),
                                in1=eac[:, 8 * g:8 * g + 8].unsqueeze(2).to_broadcast([128, 8, 64]), op=ALU.mult),
                              [pok, ("sm", t)], [("yy", t)])
                    else:
                        sample_states(t, c0, av, eac, dtx)
                    for g in range(2):
                        py, pyk = bank()
                        PE(lambda e, g=g, py=py: e.matmul(py[:, :], lhsT=identb[:], rhs=xDs[:, g * 512:(g + 1) * 512],
                                                          start=True, stop=False), ["identb", "xDs"], [pyk])
                        for h8 in range(8):
                            h = g * 8 + h8
                            PE(lambda e, h=h, h8=h8, py=py: e.matmul(py[:, h8 * 64:(h8 + 1) * 64], lhsT=Mm[:, h, :],
                                                                     rhs=xdt[:, h * 64:(h + 1) * 64], start=False, stop=(h8 == 7)),
                               [("Mm", h // 4), "xdt"], [pyk])
                        V(lambda e, g=g, py=py: e.tensor_tensor(out=yy[:, t, g * 512:(g + 1) * 512], in0=py[:, :],
                                                                in1=yy[:, t, g * 512:(g + 1) * 512], op=ALU.add),
                          [pyk, ("yy", t)], [("yy", t)])
                if not is_s:
                    V(lambda e: e.tensor_tensor(out=ST[:].rearrange("p (h d) -> p h d", d=64),
                                                in0=ST[:].rearrange("p (h d) -> p h d", d=64), in1=b64(cdec), op=ALU.mult),
                      ["ST", ("sm", t)], ["ST"])
                    for g in range(2):
                        pt, ptk = bank()
                        PE(lambda e, g=g, pt=pt: e.matmul(pt[:, :], lhsT=B_tm[:, t, g * 128:(g + 1) * 128], rhs=xdd[:, g * 512:(g + 1) * 512],
                                                          start=True, stop=True), [("B_tm", t), "xdd"], [ptk])
                        V(lambda e, g=g, pt=pt: e.tensor_tensor(out=ST[:, g * 512:(g + 1) * 512], in0=pt[:, :],
                                                                in1=ST[:, g * 512:(g + 1) * 512], op=ALU.add), [ptk, "ST"], ["ST"])
                    A(lambda e: e.activation(out=STb[:], in_=ST[:], func=AF.Identity), ["ST"], ["STb"])

            def sample_states(t, c0, av, eac, dtx):
                V(lambda e: e.tensor_tensor(out=ablk[:], in0=av.unsqueeze(1).to_broadcast([128, 16, 16]),
                                            in1=Bsel.unsqueeze(2).to_broadcast([128, 16, 16]), op=ALU.mult),
                  [("sm", t), "cst"], ["ablk"])
                pd, pdk = bank()
                PE(lambda e: e.matmul(pd[:, 0:256], lhsT=ones, rhs=ablk[:].rearrange("p a b -> p (a b)"), start=True, stop=True),
                   ["ablk", "cst"], [pdk])
                A(lambda e: e.activation(out=decall[:].rearrange("p a b -> p (a b)"), in_=pd[:, 0:256], func=AF.Exp),
                  [pdk], ["decall"])
                po = [(pf[4], ("pf", 4)), (pf[5], ("pf", 5))]
                slots = [stin[0], stin[1], snew]

                def names(bb):
                    return (slots[bb % 3], ("stin", bb % 3), stbf2[bb % 2], ("stbf", bb % 2), Cmk2[bb % 2], ("Cmk", bb % 2),
                            Bmk2[bb % 2], ("Bmk", bb % 2))

                def stage1(bb):
                    si, sk, sbf, sbk, cm_, cmk, bm_, bmk = names(bb)
                    DM("sp", lambda e: e.dma_start(out=si, in_=stT_d[bb]), [], [sk])
                    A(lambda e: e.activation(out=sbf[:], in_=si, func=AF.Identity), [sk], [sbk])
                    V(lambda e: e.tensor_tensor(out=cm_[:], in0=xc[:, 10:12, c0:c0 + 128],
                                                in1=colsel[:, bb, :].unsqueeze(1).to_broadcast([128, 2, 128]), op=ALU.mult),
                      [("xc", 10), ("xc", 11), "colsel"], [cmk])
                    V(lambda e: e.tensor_scalar(out=bm_[:], in0=B_tm[:, t, :], scalar1=Bsel[:, bb:bb + 1], scalar2=None,
                                                op0=ALU.mult), [("B_tm", t), "cst"], [bmk])

                def stage2(bb):
                    si, sk, sbf, sbk, cm_, cmk, bm_, bmk = names(bb)
                    for g in range(2):
                        PE(lambda e, g=g: e.matmul(po[g][0][:, :], lhsT=cm_[:, g, :], rhs=sbf[:, g * 512:(g + 1) * 512],
                                                   start=(bb == 0), stop=(bb == 15)), [cmk, sbk], [po[g][1]])
                    V(lambda e: e.tensor_tensor(out=si.rearrange("p (h d) -> p h d", d=64),
                                                in0=si.rearrange("p (h d) -> p h d", d=64),
                                                in1=decall[:, bb, :].unsqueeze(2).to_broadcast([128, 16, 64]), op=ALU.mult),
                      [sk, "decall"], [sk])
                    for g in range(2):
                        pt, ptk = bank()
                        PE(lambda e, g=g, pt=pt: e.matmul(pt[:, :], lhsT=bm_[:, g * 128:(g + 1) * 128], rhs=xdd[:, g * 512:(g + 1) * 512],
                                                          start=True, stop=True), [bmk, "xdd"], [ptk])
                        V(lambda e, g=g, pt=pt: e.tensor_tensor(out=si[:, g * 512:(g + 1) * 512], in0=pt[:, :],
                                                                in1=si[:, g * 512:(g + 1) * 512], op=ALU.add), [ptk, sk], [sk])
                    DM("sp", lambda e: e.dma_start(out=ssmsT_d[bb], in_=si), [sk], [])

                for bb in range(17):
                    if bb < 16:
                        stage1(bb)
                    if bb > 0:
                        stage2(bb - 1)
                for g in range(2):
                    V(lambda e, g=g: e.tensor_tensor(out=yy[:, t, g * 512:(g + 1) * 512].rearrange("p (h d) -> p h d", d=64),
                                                     in0=po[g][0][:, :].rearrange("p (h d) -> p h d", d=64),
                                                     in1=eac[:, 8 * g:8 * g + 8].unsqueeze(2).to_broadcast([128, 8, 64]), op=ALU.mult),
                      [po[g][1], ("sm", t)], [("yy", t)])

            gens = [ssd_tile(t, has_s and t == ntl - 1) for t in range(ntl)]
            active = list(gens)
            while active:
                for g_ in list(active):
                    if next(g_) == "end1":
                        active.remove(g_)
            for g_ in gens:
                for _ in g_:
                    pass
                hook()

            if not full:
                return
            if has_s:
                S.barrier(lambda e: e.memset(dummy[:], 0.0))

            def z_evac(cg, t, bk, bkey):
                A(lambda e: e.activation(out=gut[:], in_=bk[:, 0:512], func=AF.Silu), [bkey], ["gut"])
                V(lambda e: e.tensor_tensor(out=yy[:, t, cg * 512:(cg + 1) * 512], in0=yy[:, t, cg * 512:(cg + 1) * 512], in1=gut[:],
                                            op=ALU.mult), ["gut", ("yy", t)], [("yy", t)])

            tok_linear(2, z_evac)

            for t in range(ntl):
                for g in range(2):
                    A(lambda e, t=t, g=g: e.activation(out=gut[:], in_=yy[:, t, g * 512:(g + 1) * 512], func=AF.Square,
                                                       accum_out=ss2t[:, t, g:g + 1]), [("yy", t)], ["gut", ("ss2", t)])
            for t in range(ntl):
                V(lambda e, t=t: e.tensor_scalar(out=ss2t[:, t, 2:4], in0=ss2t[:, t, 0:2], scalar1=1.0 / 512, scalar2=EPS,
                                                 op0=ALU.mult, op1=ALU.add), [("ss2", t)], [("ss2", t)])
            for t in range(ntl):
                A(lambda e, t=t: e.activation(out=ss2t[:, t, 2:4], in_=ss2t[:, t, 2:4], func=AF.Sqrt), [("ss2", t)], [("ss2", t)])
            for t in range(ntl):
                V(lambda e, t=t: e.reciprocal(out=ss2t[:, t, 2:4], in_=ss2t[:, t, 2:4]), [("ss2", t)], [("ss2", t)])
            for t in range(ntl):
                c0 = t * 128
                for g in range(2):
                    V(lambda e, t=t, g=g: e.tensor_scalar(out=ysb[:, g * 512:(g + 1) * 512], in0=yy[:, t, g * 512:(g + 1) * 512],
                                                          scalar1=ss2t[:, t, 2 + g:3 + g], scalar2=None, op0=ALU.mult),
                      [("yy", t), ("ss2", t)], ["ysb"])
                pb, pbkey = bbank()
                for ch in range(8):
                    PE(lambda e, ch=ch, pb=pb: e.transpose(out=pb[:, ch * 128:(ch + 1) * 128], in_=ysb[:, ch * 128:(ch + 1) * 128],
                                                          identity=identb[:]), ["ysb", "identb"], [pbkey])
                V(lambda e, c0=c0, pb=pb: e.tensor_tensor(out=mixT[:, 0:8, c0:c0 + 128], in0=pb[:, :].rearrange("p (c t) -> p c t", t=128),
                                                          in1=pcols[:, PC_NG:PC_NG + 8].unsqueeze(2).to_broadcast([128, 8, 128]), op=ALU.mult),
                  [pbkey, "pcols"], [("mixT", c) for c in range(8)])

            for t in range(ntl):
                for g in range(2):
                    V(lambda e, t=t, g=g: e.bn_stats(out=bstt[:, t, g * 6:(g + 1) * 6], in_=gv[:, t, g * 512:(g + 1) * 512]),
                      [("gv", t)], [("bst", t)])
            for t in range(ntl):
                V(lambda e, t=t: e.bn_aggr(out=bmvt[:, t, :], in_=bstt[:, t, :].rearrange("p (a b) -> p a b", b=6)),
                  [("bst", t)], [("bmv", t)])
            for t in range(ntl):
                V(lambda e, t=t: e.tensor_scalar(out=bmvt[:, t, 1:2], in0=bmvt[:, t, 1:2], scalar1=EPS, scalar2=None, op0=ALU.add),
                  [("bmv", t)], [("bmv", t)])
            for t in range(ntl):
                A(lambda e, t=t: e.activation(out=bmvt[:, t, 1:2], in_=bmvt[:, t, 1:2], func=AF.Sqrt), [("bmv", t)], [("bmv", t)])
            for t in range(ntl):
                V(lambda e, t=t: e.reciprocal(out=bmvt[:, t, 1:2], in_=bmvt[:, t, 1:2]), [("bmv", t)], [("bmv", t)])
            for t in range(ntl):
                is_s = has_s and t == ntl - 1
                V(lambda e, t=t: e.tensor_scalar(out=gv[:, t, :], in0=gv[:, t, :], scalar1=bmvt[:, t, 0:1], scalar2=bmvt[:, t, 1:2],
                                                 op0=ALU.subtract, op1=ALU.mult), [("gv", t), ("bmv", t)], [("gv", t)])
                V(lambda e, t=t: e.tensor_tensor(out=gv[:, t, :], in0=gv[:, t, :], in1=prow[:, PR_GG:PR_GG + 1024], op=ALU.mult),
                  [("gv", t), "prow"], [("gv", t)])
                V(lambda e, t=t: e.tensor_tensor(out=gv[:, t, :], in0=gv[:, t, :], in1=prow[:, PR_GB:PR_GB + 1024], op=ALU.add),
                  [("gv", t), "prow"], [("gv", t)])
                if is_s:
                    DM("sp", lambda e, t=t: e.dma_start(out=vs_d, in_=gv[:, t, :]), [("gv", t)], [])
                A(lambda e, t=t: e.activation(out=vnb[:], in_=gv[:, t, :], func=AF.Identity), [("gv", t)], ["vnb"])
                wb = 8 if is_s else 0
                for g in range(2):
                    pm, pmk = bank()
                    for h4 in range(4):
                        h = g * 4 + h4
                        PE(lambda e, h=h, h4=h4, pm=pm, wb=wb: e.matmul(pm[:, h4 * 128:(h4 + 1) * 128], lhsT=WT[:, wb + h, :],
                                                                       rhs=vnb[:, h * 128:(h + 1) * 128], start=True, stop=True),
                           ["WT", "vnb"], [pmk])
                    V(lambda e, t=t, g=g, pm=pm, wb=wb: e.tensor_tensor(
                        out=gv[:, t, g * 512:(g + 1) * 512].rearrange("p (h d) -> p h d", d=128),
                        in0=pm[:, :].rearrange("p (h d) -> p h d", d=128),
                        in1=bsT[:, wb + g * 4:wb + g * 4 + 4].unsqueeze(2).to_broadcast([128, 4, 128]), op=ALU.add),
                      [pmk, "bsT", ("gv", t)], [("gv", t)])

            def u_evac(cg, t, bk, bkey):
                A(lambda e: e.activation(out=gut[:], in_=bk[:, 0:512], func=AF.Gelu_apprx_tanh), [bkey], ["gut"])
                V(lambda e: e.tensor_tensor(out=ysb[:, cg * 512:(cg + 1) * 512], in0=gut[:], in1=gv[:, t, cg * 512:(cg + 1) * 512],
                                            op=ALU.mult), ["gut", ("gv", t)], ["ysb"])
                pb, pbkey = bbank()
                for c4 in range(4):
                    ch = cg * 4 + c4
                    PE(lambda e, ch=ch, c4=c4, pb=pb: e.transpose(out=pb[:, c4 * 128:(c4 + 1) * 128], in_=ysb[:, ch * 128:(ch + 1) * 128],
                                                                  identity=identb[:]), ["ysb", "identb"], [pbkey])
                c0 = t * 128
                A(lambda e, pb=pb: e.activation(out=mixT[:, 8 + cg * 4:12 + cg * 4, c0:c0 + 128],
                                                in_=pb[:, 0:512].rearrange("p (c t) -> p c t", t=128), func=AF.Identity),
                  [pbkey], [("mixT", 8 + cg * 4 + i) for i in range(4)])

            tok_linear(6, u_evac)

            S.barrier(lambda e: e.memset(dummy[:], 0.0))
            pool_n[0] = 6
            def res_evac(Gt, gk):
                def ev(ch, bk, bkey):
                    if TBp > 0:
                        V(lambda e: e.scalar_tensor_tensor(out=xA[:, ch, 0:TBp], in0=bk[:, 0:TBp], scalar=Gt[:, ch, 0:1],
                                                           in1=xA[:, ch, 0:TBp], op0=ALU.mult, op1=ALU.add),
                          [bkey, gk, ("xA", ch)], [("xA", ch)])
                    if has_s:
                        r_ = rl[ch % 2]
                        rk_ = ("rl", ch % 2)
                        V(lambda e: e.tensor_tensor(out=r_[:, 0:128].rearrange("p (b t) -> p b t", t=8),
                                                    in0=bk[:, TBp:TB].rearrange("p (b t) -> p b t", t=8),
                                                    in1=Gt[:, ch, 1:17].unsqueeze(2).to_broadcast([128, 16, 8]), op=ALU.mult),
                          [bkey, gk], [rk_])
                        V(lambda e: e.tensor_tensor(out=xA[:, ch, TBp:TB], in0=r_[:, 0:128], in1=xA[:, ch, TBp:TB], op=ALU.add),
                          [rk_, ("xA", ch)], [("xA", ch)])
                return ev

            feat_linear(wout_d, 4, 16, mixT, "mixT", TB, res_evac(Gm, "Gm"))

            ln_stats(xA, TB, "xA")
            def fnD(c, t, tk):
                A(lambda e, c=c, t=t: e.activation(out=xA[:, c, 0:TB], in_=t[:, 0:TB], func=AF.Identity,
                                                   scale=agin[:, 32 + c:33 + c], bias=agin[:, 48 + c:49 + c]),
                  [tk, "agin"], [("xA", c)])
                mod_h(t, tk, c, TBp, TB, sclh2, bish2, "sclh2", "bish2")

            ln_apply(xA, TB, "xA", fnD)

            def ff1_evac(ch, bk, bkey):
                r_ = rl[ch % 2]
                rk_ = ("rl", ch % 2)
                A(lambda e: e.activation(out=r_[:, 0:TB], in_=bk[:, 0:TB], func=AF.Relu), [bkey], [rk_])
                V(lambda e: e.tensor_tensor(out=hid[:, ch, 0:TB], in0=r_[:, 0:TB], in1=r_[:, 0:TB], op=ALU.mult),
                  [rk_], [("hid", ch)])

            feat_linear(wff1_d, 16, 16, hT, "hT", TB, ff1_evac)
            feat_linear(wff2_d, 4, 64, hid, "hid", TB, res_evac(Gf, "Gf"))
            ln_stats(xA, TB, "xA")
            def fnG(c, t, tk):
                A(lambda e, c=c, t=t: e.activation(out=xA[:, c, 0:TB], in_=t[:, 0:TB], func=AF.Identity,
                                                   scale=pcols[:, PC_GFFN + c:PC_GFFN + c + 1], bias=pcols[:, PC_BFFN + c:PC_BFFN + c + 1]),
                  [tk, "pcols"], [("xA", c)])
                DM("sp", lambda e, c=c: e.dma_start(out=yT_d[c * 128:(c + 1) * 128, col0:col0 + TB], in_=xA[:, c, 0:TB]),
                   [("xA", c)], [])

            ln_apply(xA, TB, "xA", fnG)

        t0 = 0
        while t0 < 8:
            n = min(nt, 8 - t0)
            run_block(xpT_d, t0 * 128, n, False, "prefix", hook=lambda: emit_mod(1),
                      pre_hook=(setup_mod1 if t0 == 0 else None))
            t0 += n
        V(lambda e: e.tensor_scalar(out=ST[:], in0=ST[:], scalar1=gate[:, 0:1], scalar2=None, op0=ALU.mult), ["ST", "gate"], ["ST"])
        A(lambda e: e.activation(out=STb[:], in_=ST[:], func=AF.Identity), ["ST"], ["STb"])
        V(lambda e: e.tensor_scalar(out=tail[:].rearrange("p a b -> p (a b)"), in0=tail[:].rearrange("p a b -> p (a b)"),
                                    scalar1=gate[:, 0:1], scalar2=None, op0=ALU.mult),
          [("tail", c) for c in range(12)] + ["gate"], [("tail", c) for c in range(12)])
        emit_mod(24)
        V(lambda e: e.tensor_scalar(out=Gm[:], in0=modT[:, 32:48, :], scalar1=1.0, scalar2=None, op0=ALU.add),
          [("modT", 2)], ["Gm"])
        V(lambda e: e.tensor_scalar(out=ctmp, in0=modT[:, 64:80, :], scalar1=1.0, scalar2=None, op0=ALU.add),
          [("modT", 4), "sclh", "ctmp2"], ["ctmp"])
        V(lambda e: e.tensor_tensor(out=sclh2[:], in0=ctmp, in1=bc17(PC_GMIX), op=ALU.mult), ["ctmp", "pcols"], ["sclh2"])
        V(lambda e: e.tensor_tensor(out=ctmp2, in0=ctmp, in1=bc17(PC_BMIX), op=ALU.mult), ["ctmp", "pcols"], ["ctmp2"])
        V(lambda e: e.tensor_tensor(out=bish2[:], in0=ctmp2, in1=modT[:, 48:64, :], op=ALU.add),
          ["ctmp2", ("modT", 3)], ["bish2"])
        V(lambda e: e.tensor_scalar(out=Gf[:], in0=modT[:, 80:96, :], scalar1=1.0, scalar2=None, op0=ALU.add),
          [("modT", 5)], ["Gf"])

        S.barrier(lambda e: e.memset(dummy[:], 0.0))
        tiles = [("P", i) for i in range(8)] + [("S", 0)]
        i = 0
        while i < 9:
            grp = tiles[i:i + nt]
            ntp = sum(1 for g in grp if g[0] == "P")
            has_s = any(g[0] == "S" for g in grp)
            run_block(xT_d, i * 128, ntp, has_s, "full")
            if ntp > 0 and grp[ntp - 1] == ("P", 7):
                DM("sp", lambda e: e.dma_start(out=ssmT_d, in_=ST[:]), ["ST"], [])
                DM("sp", lambda e: e.dma_start(out=convT_d, in_=tail[:]), [("tail", c) for c in range(12)], [])
            i += nt
        DM("sp", lambda e: e.dma_start(out=convsT_d, in_=convs[:]), ["convs"], [])
        allq = [op for q in ["sp", "pool"] for op in S.ops[q] if op.is_dma]
        fin = S.add("sp", lambda e: e.engine_nop() if hasattr(e, "engine_nop") else e.sem_inc(sems["sp"], 0))
        last = {}
        for op in allq:
            last[op.sem] = op
        fin.deps.extend(last.values())
        S.finalize()
        with nc.Block() as block:
            S.emit(block, sems, dsems)
    return nc


def _tile_w(w, groups, cgw):
    K = w.shape[0]
    out = np.empty((len(groups), K // 256, 128, 2, cgw), np.float32)
    for gi, c0 in enumerate(groups):
        blk = w[:, c0:c0 + cgw].reshape(K // 256, 2, 128, cgw)
        out[gi] = blk.transpose(0, 2, 1, 3)
    return out


_CACHE = {}


def kernel(x_prompt, x_sample, state_ssm, state_conv, c_prompt, c_sample, ln_in_g, ln_in_b,
           w_mod, b_mod, w_in, conv_w, conv_b, dt_bias, a_log, d_skip, ssd_norm_g, gm_ln_g, gm_ln_b,
           gm_w_s, gm_b_s, w_out, ln_mix_g, ln_mix_b, w_ff1, w_ff2, ln_ffn_g, ln_ffn_b):
    f = np.float32
    a = lambda v: np.ascontiguousarray(np.asarray(v, dtype=f))
    x_prompt, x_sample, state_ssm, state_conv = a(x_prompt), a(x_sample), a(state_ssm), a(state_conv)
    c_prompt, c_sample = a(c_prompt), a(c_sample)
    if "nc" not in _CACHE:
        _CACHE["nc"] = build_program(NT)
    nc = _CACHE["nc"]

    def col(v, n):
        return a(v).reshape(n, 128).T

    wmod_t = _tile_w(a(w_mod)[0], [i * 512 for i in range(24)], 512)
    win = a(w_in)[0]
    wina_t = _tile_w(win, [1024, 1536, 0, 512, 3600, 4112, 2576, 3088], 512)
    winb_t = _tile_w(win, [2048], 528)
    wout_t = _tile_w(a(w_out)[0], [i * 512 for i in range(4)], 512)
    wff1_t = _tile_w(a(w_ff1)[0], [i * 512 for i in range(16)], 512)
    wff2_t = _tile_w(a(w_ff2)[0], [i * 512 for i in range(4)], 512)

    pcols = np.zeros((128, PC_N), f)
    pcols[:, PC_GIN:PC_GIN + 16] = col(ln_in_g, 16)
    pcols[:, PC_BIN:PC_BIN + 16] = col(ln_in_b, 16)
    pcols[:, PC_BMOD:PC_BMOD + 96] = col(a(b_mod)[0], 96)
    cw = a(conv_w)[0]
    pcols[:, PC_CW:PC_CW + 48] = cw.reshape(4, 12, 128).transpose(2, 1, 0).reshape(128, 48)
    pcols[:, PC_CB:PC_CB + 12] = col(a(conv_b)[0], 12)
    pcols[:, PC_NG:PC_NG + 8] = col(a(ssd_norm_g)[0], 8)
    pcols[:, PC_GMIX:PC_GMIX + 16] = col(a(ln_mix_g)[0], 16)
    pcols[:, PC_BMIX:PC_BMIX + 16] = col(a(ln_mix_b)[0], 16)
    pcols[:, PC_GFFN:PC_GFFN + 16] = col(a(ln_ffn_g)[0], 16)
    pcols[:, PC_BFFN:PC_BFFN + 16] = col(a(ln_ffn_b)[0], 16)
    prow = np.zeros((128, PR_N), f)
    prow[:, PR_DTB:PR_DTB + 16] = a(dt_bias)[0][None]
    prow[:, PR_ALOG:PR_ALOG + 16] = a(a_log)[0][None]
    prow[:, PR_DSK:PR_DSK + 16] = a(d_skip)[0][None]
    prow[:, PR_GG:PR_GG + 1024] = a(gm_ln_g)[0][None]
    prow[:, PR_GB:PR_GB + 1024] = a(gm_ln_b)[0][None]
    bs = a(gm_b_s)[0]
    idx = np.arange(128)
    bsT = np.concatenate([bs.T, bs[:, idx % 8].T], axis=1)
    ws = a(gm_w_s)[0]
    wsT = np.empty((128, 16, 128), f)
    wsT[:, 0:8, :] = ws.transpose(2, 0, 1)
    wsT[:, 8:16, :] = ws[:, idx % 8][:, :, idx % 8].transpose(2, 0, 1)
    cst = np.zeros((128, CS_N), f)
    ii = idx[:, None]
    jj = idx[None, :]
    cst[:, CS_ID:CS_ID + 128] = np.eye(128)
    cst[:, CS_U:CS_U + 128] = (ii > jj)
    cst[:, CS_L:CS_L + 128] = (ii <= jj)
    cst[:, CS_ONE:CS_ONE + 128] = 1.0
    cst[:, CS_LB:CS_LB + 128] = (ii <= jj) & (ii // 8 == jj // 8)
    cst[:, CS_BO:CS_BO + 128] = (ii // 8 == jj // 8)
    cst[:, CS_BSEL:CS_BSEL + 16] = (ii // 8 == np.arange(16)[None, :])
    colsel = np.broadcast_to((np.arange(16)[:, None] == (idx // 8)[None, :])[None], (128, 16, 128)).astype(f)

    in_maps = []
    for c in range(8):
        b, half = c // 2, c % 2
        xo = x_prompt[b, half * 1024:(half + 1) * 1024]
        xs = x_sample[16 * c:16 * c + 16].reshape(128, D)
        xT = np.ascontiguousarray(np.concatenate([xo, xs], 0).T)
        xpT = np.ascontiguousarray(x_prompt[b, 0:1024].T)
        cT = np.concatenate([c_prompt[b][None], c_sample[16 * c:16 * c + 16]], 0)
        cT = np.ascontiguousarray(cT.reshape(17, 16, 128).transpose(2, 1, 0))
        stT = np.ascontiguousarray(state_ssm[0, 16 * c:16 * c + 16].reshape(16, 1024, 128).transpose(0, 2, 1))
        cvT = np.ascontiguousarray(state_conv[0, 16 * c:16 * c + 16].reshape(16, 3, 12, 128).transpose(3, 2, 0, 1))
        in_maps.append({
            "xT": xT, "xpT": xpT, "gate": np.full((128, 1), float(half), f), "cT": cT,
            "wmod": wmod_t, "wina": wina_t, "winb": winb_t, "wout": wout_t, "wff1": wff1_t, "wff2": wff2_t,
            "pcols": pcols, "prow": prow, "bsT": np.ascontiguousarray(bsT), "wsT": wsT, "cst": cst, "colsel": colsel,
            "stT": stT, "cvT": cvT,
        })
    res = run_bass_kernel_spmd(nc, in_maps, core_ids=list(range(8)))
    R = res.results
    y_prompt = np.empty((4, 2048, D), f)
    y_sample = np.empty((128, 8, D), f)
    ssm_p = np.empty((1, 4, 16, 64, 128), f)
    conv_p = np.empty((1, 4, 3, 1536), f)
    ssm_s = np.empty((1, 128, 16, 64, 128), f)
    conv_s = np.empty((1, 128, 3, 1536), f)
    v_s = np.empty((1, 128, 8, 1024), f)
    for c in range(8):
        b, half = c // 2, c % 2
        yT = np.asarray(R[c]["yT"])
        y_prompt[b, half * 1024:(half + 1) * 1024] = yT[:, 0:1024].T
        y_sample[16 * c:16 * c + 16] = yT[:, 1024:1152].T.reshape(16, 8, D)
        if half == 1:
            ssm_p[0, b] = np.asarray(R[c]["ssmT"]).T.reshape(16, 64, 128)
            conv_p[0, b] = np.asarray(R[c]["convT"]).transpose(2, 1, 0).reshape(3, 1536)
        ssm_s[0, 16 * c:16 * c + 16] = np.asarray(R[c]["ssmsT"]).transpose(0, 2, 1).reshape(16, 16, 64, 128)
        conv_s[0, 16 * c:16 * c + 16] = np.asarray(R[c]["convsT"]).transpose(2, 3, 1, 0).reshape(16, 3, 1536)
        v_s[0, 16 * c:16 * c + 16] = np.asarray(R[c]["vs"]).reshape(16, 8, 1024)
    return (y_prompt, y_sample, ssm_p, conv_p, ssm_s, conv_s, v_s)
```
